# Optimizing a Trainium2 kernel written in Bass

```python
import math
import jax, jax.numpy as jnp
from jax import lax
import numpy as np

D_MODEL = 1024
BATCH = 4
SEQ = 8192
DEPTH = 4

MIX_WIDTH = 2 * D_MODEL
N_GROUPS = 4
GROUP_WIDTH = MIX_WIDTH // N_GROUPS
N_IN_SPLITS = 10
IN_WIDTH = N_IN_SPLITS * GROUP_WIDTH
EPS = 1e-6
S5_CH = 16
S5_GROUPS = GROUP_WIDTH // S5_CH
S5_STATE = 64
S5_DT_MIN = 1e-3
S5_DT_MAX = 1e-1
DA_HEADS = 4
DA_HEAD_DIM = GROUP_WIDTH // DA_HEADS // 2
DA_V_DIM = 2 * DA_HEAD_DIM
ROPE_THETA = 500000.0
ROPE_DIMS = DA_HEAD_DIM // 4
Q_BLOCK = 128
LRU_BLOCKS = 8
LRU_BLOCK_DIM = GROUP_WIDTH // LRU_BLOCKS
LRU_C = 8.0
CONV_WIDTH = 4
MEM_TOKENS = 256
MEM_HEADS = 4
MEM_HEAD_DIM = GROUP_WIDTH // MEM_HEADS

kernel_name = "hymba_s5_diffattn_rglru_memory"

F32 = jnp.float32


def rms_norm(x, g):
    xf = x.astype(F32)
    y = xf * lax.rsqrt(jnp.mean(xf * xf, axis=-1, keepdims=True) + EPS)
    return (y * g.astype(F32)).astype(x.dtype)


def rope_partial(x, cos, sin):
    xf = x.astype(F32)
    half = ROPE_DIMS // 2
    x1, x2, xp = xf[..., :half], xf[..., half:ROPE_DIMS], xf[..., ROPE_DIMS:]
    out = jnp.concatenate([x1 * cos - x2 * sin, x2 * cos + x1 * sin, xp], axis=-1)
    return out.astype(x.dtype)


def s5_mixer(u, lam_re, lam_im, log_dt, b_re, b_im, c_re, c_im, d_skip, w_glu):
    B, L, _ = u.shape
    uf = u.astype(F32).reshape(B, L, S5_GROUPS, S5_CH)
    dt = jnp.exp(log_dt.astype(F32))[:, None]
    lr, li = lam_re.astype(F32), lam_im.astype(F32)
    mag = jnp.exp(lr * dt)
    abar_re = mag * jnp.cos(li * dt)
    abar_im = mag * jnp.sin(li * dt)
    den = lr * lr + li * li
    nr, ni = abar_re - 1.0, abar_im
    f_re = (nr * lr + ni * li) / den
    f_im = (ni * lr - nr * li) / den
    br, bi = b_re.astype(F32), b_im.astype(F32)
    bb_re = f_re[..., None] * br - f_im[..., None] * bi
    bb_im = f_re[..., None] * bi + f_im[..., None] * br
    bu_re = jnp.einsum('blgh,gph->blgp', uf, bb_re)
    bu_im = jnp.einsum('blgh,gph->blgp', uf, bb_im)
    a_re = jnp.broadcast_to(abar_re, bu_re.shape)
    a_im = jnp.broadcast_to(abar_im, bu_im.shape)

    def combine(e1, e2):
        a1r, a1i, b1r, b1i = e1
        a2r, a2i, b2r, b2i = e2
        return (a2r * a1r - a2i * a1i,
                a2r * a1i + a2i * a1r,
                a2r * b1r - a2i * b1i + b2r,
                a2r * b1i + a2i * b1r + b2i)

    _, _, xr, xi = lax.associative_scan(combine, (a_re, a_im, bu_re, bu_im), axis=1)
    y = (jnp.einsum('blgp,ghp->blgh', xr, c_re.astype(F32))
         - jnp.einsum('blgp,ghp->blgh', xi, c_im.astype(F32))
         + d_skip.astype(F32) * uf)
    y = jax.nn.gelu(y.reshape(B, L, GROUP_WIDTH)).astype(u.dtype)
    ga, gb = jnp.split(y @ w_glu, 2, axis=-1)
    return ga * jax.nn.sigmoid(gb)


def diff_attention(q, k, v, lam):
    B, L = q.shape[:2]
    nb = L // Q_BLOCK
    scale = DA_HEAD_DIM ** -0.5
    k1, k2 = k[..., 0, :], k[..., 1, :]
    qb = (q * scale).reshape(B, nb, Q_BLOCK, DA_HEADS, 2, DA_HEAD_DIM)
    qb = jnp.moveaxis(qb, 1, 0)
    kpos = jnp.arange(L)

    def block(args):
        qblk, i = args
        qpos = i * Q_BLOCK + jnp.arange(Q_BLOCK)
        mask = kpos[None, :] <= qpos[:, None]
        s1 = jnp.einsum('bqhd,bkhd->bhqk', qblk[..., 0, :], k1).astype(F32)
        s2 = jnp.einsum('bqhd,bkhd->bhqk', qblk[..., 1, :], k2).astype(F32)
        s1 = jnp.where(mask, s1, -1e30)
        s2 = jnp.where(mask, s2, -1e30)
        p = jax.nn.softmax(s1, axis=-1) - lam * jax.nn.softmax(s2, axis=-1)
        return jnp.einsum('bhqk,bkhd->bqhd', p.astype(v.dtype), v)

    out = lax.map(block, (qb, jnp.arange(nb)))
    return jnp.moveaxis(out, 0, 1).reshape(B, L, DA_HEADS, DA_V_DIM)


def rglru_mixer(x, conv_w, conv_b, w_a, b_a, w_x, b_x, lam):
    B, L, W = x.shape
    xc = lax.conv_general_dilated(
        x, conv_w[:, None, :], window_strides=(1,), padding=((CONV_WIDTH - 1, 0),),
        dimension_numbers=('NWC', 'WIO', 'NWC'), feature_group_count=W) + conv_b
    xf = xc.astype(F32)
    xb = xf.reshape(B, L, LRU_BLOCKS, LRU_BLOCK_DIM)
    r = jax.nn.sigmoid(jnp.einsum('blni,nij->blnj', xb, w_a.astype(F32)).reshape(B, L, W)
                       + b_a.astype(F32))
    i = jax.nn.sigmoid(jnp.einsum('blni,nij->blnj', xb, w_x.astype(F32)).reshape(B, L, W)
                       + b_x.astype(F32))
    log_a = -LRU_C * r * jax.nn.softplus(-lam.astype(F32))
    a = jnp.exp(log_a)
    mult = jnp.sqrt(-jnp.expm1(2.0 * log_a))
    b = mult * (i * xf)

    def combine(e1, e2):
        a1, b1 = e1
        a2, b2 = e2
        return a2 * a1, a2 * b1 + b2

    _, h = lax.associative_scan(combine, (a, b), axis=1)
    return h.astype(x.dtype)


def memory_attention(q, mem_n, w_mem_kv):
    B, L = q.shape[:2]
    k, v = jnp.split(mem_n @ w_mem_kv, 2, axis=-1)
    k = k.reshape(B, -1, MEM_HEADS, MEM_HEAD_DIM)
    v = v.reshape(B, -1, MEM_HEADS, MEM_HEAD_DIM)
    qh = q.reshape(B, L, MEM_HEADS, MEM_HEAD_DIM)
    s = jnp.einsum('blhd,bmhd->bhlm', qh, k).astype(F32) * (MEM_HEAD_DIM ** -0.5)
    p = jax.nn.softmax(s, axis=-1)
    return jnp.einsum('bhlm,bmhd->blhd', p.astype(v.dtype), v).reshape(B, L, GROUP_WIDTH)


def setup_inputs(seed: int = 0) -> dict:
    key = jax.random.key(seed)
    ks = jax.random.split(key, 32)

    def nrm(k, shape, scale):
        return jax.random.normal(k, shape, F32) * scale

    x = nrm(ks[0], (BATCH, SEQ, D_MODEL), 1.0)
    mem = nrm(ks[1], (BATCH, MEM_TOKENS, D_MODEL), 1.0)
    positions = jnp.broadcast_to(jnp.arange(SEQ, dtype=jnp.int32)[None, :], (BATCH, SEQ))
    norm_g = 1.0 + nrm(ks[2], (DEPTH, D_MODEL), 0.02)
    w_in = nrm(ks[3], (DEPTH, D_MODEL, IN_WIDTH), D_MODEL ** -0.5)
    w_out = nrm(ks[4], (DEPTH, MIX_WIDTH, D_MODEL), MIX_WIDTH ** -0.5)
    n = jnp.arange(S5_STATE, dtype=F32)
    s5_lambda_re = -0.5 + nrm(ks[5], (DEPTH, S5_GROUPS, S5_STATE), 0.01)
    s5_lambda_im = math.pi * n + nrm(ks[6], (DEPTH, S5_GROUPS, S5_STATE), 0.01)
    s5_log_dt = jax.random.uniform(ks[7], (DEPTH, S5_GROUPS), F32,
                                   math.log(S5_DT_MIN), math.log(S5_DT_MAX))
    bscale = (2.0 * S5_CH) ** -0.5
    cscale = (2.0 * S5_STATE) ** -0.5
    s5_b_re = nrm(ks[8], (DEPTH, S5_GROUPS, S5_STATE, S5_CH), bscale)
    s5_b_im = nrm(ks[9], (DEPTH, S5_GROUPS, S5_STATE, S5_CH), bscale)
    s5_c_re = nrm(ks[10], (DEPTH, S5_GROUPS, S5_CH, S5_STATE), cscale)
    s5_c_im = nrm(ks[11], (DEPTH, S5_GROUPS, S5_CH, S5_STATE), cscale)
    s5_d = nrm(ks[12], (DEPTH, S5_GROUPS, S5_CH), 1.0)
    s5_w_glu = nrm(ks[13], (DEPTH, GROUP_WIDTH, 2 * GROUP_WIDTH), GROUP_WIDTH ** -0.5)
    da_lambda_q1 = nrm(ks[14], (DEPTH, DA_HEAD_DIM), 0.1)
    da_lambda_k1 = nrm(ks[15], (DEPTH, DA_HEAD_DIM), 0.1)
    da_lambda_q2 = nrm(ks[16], (DEPTH, DA_HEAD_DIM), 0.1)
    da_lambda_k2 = nrm(ks[17], (DEPTH, DA_HEAD_DIM), 0.1)
    da_subln_g = 1.0 + nrm(ks[18], (DEPTH, DA_V_DIM), 0.02)
    lru_conv_w = nrm(ks[19], (DEPTH, CONV_WIDTH, GROUP_WIDTH), CONV_WIDTH ** -0.5)
    lru_conv_b = nrm(ks[20], (DEPTH, GROUP_WIDTH), 0.01)
    lru_w_a = nrm(ks[21], (DEPTH, LRU_BLOCKS, LRU_BLOCK_DIM, LRU_BLOCK_DIM), LRU_BLOCK_DIM ** -0.5)
    lru_b_a = nrm(ks[22], (DEPTH, GROUP_WIDTH), 0.01)
    lru_w_x = nrm(ks[23], (DEPTH, LRU_BLOCKS, LRU_BLOCK_DIM, LRU_BLOCK_DIM), LRU_BLOCK_DIM ** -0.5)
    lru_b_x = nrm(ks[24], (DEPTH, GROUP_WIDTH), 0.01)
    ua = jax.random.uniform(ks[25], (DEPTH, GROUP_WIDTH), F32, 0.9, 0.999)
    sa = ua ** (1.0 / LRU_C)
    lru_lambda = jnp.log(sa) - jnp.log1p(-sa)
    mem_norm_g = 1.0 + nrm(ks[26], (DEPTH, D_MODEL), 0.02)
    w_mem_kv = nrm(ks[27], (DEPTH, D_MODEL, 2 * GROUP_WIDTH), D_MODEL ** -0.5)
    final_norm_g = 1.0 + nrm(ks[28], (D_MODEL,), 0.02)
    return {"x": x, "mem": mem, "positions": positions, "norm_g": norm_g,
            "w_in": w_in, "w_out": w_out,
            "s5_lambda_re": s5_lambda_re, "s5_lambda_im": s5_lambda_im, "s5_log_dt": s5_log_dt,
            "s5_b_re": s5_b_re, "s5_b_im": s5_b_im, "s5_c_re": s5_c_re, "s5_c_im": s5_c_im,
            "s5_d": s5_d, "s5_w_glu": s5_w_glu,
            "da_lambda_q1": da_lambda_q1, "da_lambda_k1": da_lambda_k1,
            "da_lambda_q2": da_lambda_q2, "da_lambda_k2": da_lambda_k2, "da_subln_g": da_subln_g,
            "lru_conv_w": lru_conv_w, "lru_conv_b": lru_conv_b, "lru_w_a": lru_w_a,
            "lru_b_a": lru_b_a, "lru_w_x": lru_w_x, "lru_b_x": lru_b_x, "lru_lambda": lru_lambda,
            "mem_norm_g": mem_norm_g, "w_mem_kv": w_mem_kv, "final_norm_g": final_norm_g}


def reference(x, mem, positions, norm_g, w_in, w_out,
              s5_lambda_re, s5_lambda_im, s5_log_dt, s5_b_re, s5_b_im, s5_c_re, s5_c_im,
              s5_d, s5_w_glu,
              da_lambda_q1, da_lambda_k1, da_lambda_q2, da_lambda_k2, da_subln_g,
              lru_conv_w, lru_conv_b, lru_w_a, lru_b_a, lru_w_x, lru_b_x, lru_lambda,
              mem_norm_g, w_mem_kv, final_norm_g):
    B, L, _ = x.shape
    inv_freq = ROPE_THETA ** (-jnp.arange(0, ROPE_DIMS, 2, dtype=F32) / ROPE_DIMS)
    ang = positions.astype(F32)[..., None] * inv_freq
    cos = jnp.cos(ang)[:, :, None, None, :]
    sin = jnp.sin(ang)[:, :, None, None, :]

    for layer in range(DEPTH):
        h = rms_norm(x, norm_g[layer])
        proj = h @ w_in[layer]
        a_u, a_g, qd, kd, vd, b_g, c_x, c_g, m_q, m_g = jnp.split(proj, N_IN_SPLITS, axis=-1)

        y_a = s5_mixer(a_u, s5_lambda_re[layer], s5_lambda_im[layer], s5_log_dt[layer],
                       s5_b_re[layer], s5_b_im[layer], s5_c_re[layer], s5_c_im[layer],
                       s5_d[layer], s5_w_glu[layer])

        lam_init = 0.8 - 0.6 * math.exp(-0.3 * layer)
        lam = (jnp.exp(jnp.sum(da_lambda_q1[layer].astype(F32) * da_lambda_k1[layer].astype(F32)))
               - jnp.exp(jnp.sum(da_lambda_q2[layer].astype(F32) * da_lambda_k2[layer].astype(F32)))
               + lam_init)
        q = rope_partial(qd.reshape(B, L, DA_HEADS, 2, DA_HEAD_DIM), cos, sin)
        k = rope_partial(kd.reshape(B, L, DA_HEADS, 2, DA_HEAD_DIM), cos, sin)
        v = vd.reshape(B, L, DA_HEADS, DA_V_DIM)
        o_b = diff_attention(q, k, v, lam)
        y_b = (rms_norm(o_b, da_subln_g[layer]) * (1.0 - lam_init)).reshape(B, L, GROUP_WIDTH)

        y_c = rglru_mixer(c_x, lru_conv_w[layer], lru_conv_b[layer], lru_w_a[layer],
                          lru_b_a[layer], lru_w_x[layer], lru_b_x[layer], lru_lambda[layer])

        mem_n = rms_norm(mem, mem_norm_g[layer])
        y_m = memory_attention(m_q, mem_n, w_mem_kv[layer])

        mixed = jnp.concatenate([y_a * jax.nn.silu(a_g), y_b * jax.nn.silu(b_g),
                                 y_c * jax.nn.silu(c_g), y_m * jax.nn.silu(m_g)], axis=-1)
        x = x + mixed @ w_out[layer]

    return rms_norm(x, final_norm_g)
```

```python
import numpy as np
import concourse.bass as bass
import concourse.mybir as mybir
from concourse.bass_utils import run_bass_kernel_spmd
from contextlib import ExitStack

F32 = mybir.dt.float32
BF16 = mybir.dt.bfloat16
I32 = mybir.dt.int32
AF = mybir.ActivationFunctionType
ALU = mybir.AluOpType
AX = mybir.AxisListType

ENGS = ("pe", "act", "dve", "pool", "sp")


class Buf:
    __slots__ = ("name", "w", "r", "ap", "full")

    def __init__(self, name, ap=None):
        self.name = name
        self.w = []
        self.r = []
        self.ap = ap

    def __getitem__(self, k):
        return self.ap[k]


class K:
    def __init__(self, nc):
        self.nc = nc
        self.es = ExitStack()
        self.prog = {e: [] for e in ENGS}
        self.cnt = {}
        self.sems = {}
        self.known = {e: {} for e in ENGS}
        for e in ENGS[:4]:
            self._newsem(e)
        self.dma_sems = []
        self.dma_rr = 0
        self.ninstr = 0

    def _newsem(self, key):
        self.sems[key] = self.es.enter_context(self.nc.semaphore("s_" + str(key)))
        self.cnt[key] = 0

    def sb(self, name, shape, dt):
        t = self.es.enter_context(self.nc.sbuf_tensor("sb_" + name, list(shape), dt))
        return Buf(name, t)

    def ps(self, name, shape, dt=F32):
        t = self.es.enter_context(self.nc.psum_tensor("ps_" + name, list(shape), dt))
        return Buf(name, t)

    def dram(self, name, shape, dt, kind="Internal"):
        t = self.nc.dram_tensor(name, list(shape), dt, kind=kind)
        return t.ap()

    def _waits(self, eng, reads, writes):
        need = {}
        def add(t):
            k, v = t
            if eng == "pe" and k == "pe":
                return
            if need.get(k, 0) < v:
                need[k] = v
        for b in reads:
            for t in b.w:
                add(t)
        for b in writes:
            for t in b.w:
                add(t)
            for t in b.r:
                add(t)
        kn = self.known[eng]
        out = []
        for k, v in need.items():
            if kn.get(k, 0) < v:
                kn[k] = v
                out.append((k, v))
        return out

    def _commit(self, ticket, reads, writes):
        for b in writes:
            b.w = [ticket]
            b.r = []
        for b in reads:
            if b not in writes:
                b.r.append(ticket)
                if len(b.r) > 24:
                    m = {}
                    for k, v in b.r:
                        if m.get(k, 0) < v:
                            m[k] = v
                    b.r = list(m.items())

    def op(self, eng, fn, reads=(), writes=(), inc=True):
        waits = self._waits(eng, reads, writes)
        if inc:
            self.cnt[eng] += 1
            ticket = (eng, self.cnt[eng])
            self.prog[eng].append((waits, fn, (eng, 1)))
        else:
            ticket = (eng, self.cnt[eng] + 1)
            self.prog[eng].append((waits, fn, None))
        self._commit(ticket, reads, writes)
        self.ninstr += 1

    def dma(self, out_ap, in_ap, reads=(), writes=(), q="sp", key=None):
        if key is None:
            key = "dma_" + (writes[0].name if writes and writes[0].ap is not None else reads[0].name)
        if key not in self.sems:
            self._newsem(key)
        waits = self._waits(q, reads, writes)
        prev = self.cnt[key]
        if prev > 0 and self.known[q].get(key, 0) < prev:
            self.known[q][key] = prev
            waits.append((key, prev))
        self.cnt[key] += 16
        ticket = (key, self.cnt[key])
        fn = lambda e, o=out_ap, i=in_ap: e.dma_start(out=o, in_=i)
        self.prog[q].append((waits, fn, (key, 16)))
        self._commit(ticket, reads, writes)
        self.ninstr += 1

    def dma_group(self, items, key, q="sp"):
        if key not in self.sems:
            self._newsem(key)
        prev = self.cnt[key]
        final = prev + 16 * len(items)
        first = True
        for (o, i, reads, writes) in items:
            waits = self._waits(q, reads, writes)
            if first and prev > 0 and self.known[q].get(key, 0) < prev:
                self.known[q][key] = prev
                waits.append((key, prev))
            first = False
            fn = lambda e, o=o, i=i: e.dma_start(out=o, in_=i)
            self.prog[q].append((waits, fn, (key, 16)))
            self._commit((key, final), reads, writes)
            self.ninstr += 1
        self.cnt[key] = final

    def coll(self, kind, alu, in_ap, out_ap, reads, writes):
        key = "cc%d" % len([s for s in self.sems if str(s).startswith("cc")])
        self._newsem(key)
        waits = self._waits("pool", reads, writes)
        self.cnt[key] = 1
        fn = lambda e, i=in_ap, o=out_ap: e.collective_compute(kind, alu, replica_groups=GROUPS, ins=[i], outs=[o])
        self.prog["pool"].append((waits, fn, (key, None)))
        self._commit((key, 1), reads, writes)
        self.ninstr += 1

    def finish(self, final_bufs):
        waits = self._waits("sp", final_bufs, final_bufs)
        self.prog["sp"].append((waits, None, None))
        nc = self.nc
        sems = self.sems
        prog = self.prog

        def run(e, lst):
            for waits, fn, inc in lst:
                for k, v in waits:
                    e.wait_ge(sems[k], v)
                if fn is not None:
                    ins = fn(e)
                    if inc is not None:
                        if inc[1] is None:
                            ins.then_inc(sems[inc[0]])
                        else:
                            ins.then_inc(sems[inc[0]], inc[1])

        with nc.Block() as block:
            @block.tensor
            def _(e):
                run(e, prog["pe"])

            @block.scalar
            def _(e):
                run(e, prog["act"])

            @block.vector
            def _(e):
                run(e, prog["dve"])

            @block.gpsimd
            def _(e):
                run(e, prog["pool"])

            @block.sync
            def _(e):
                run(e, prog["sp"])
        self.es.close()


D = 1024
TT = 512
NCH = 41
NH = 2
CH = 4
GROUPS = [[0, 1], [2, 3], [4, 5], [6, 7]]
NSP = 357
NCN = 518
EPS = 1e-6
TWO_PI = float(2 * np.pi)
import math


def _cols_in(h):
    G = 512
    def seg(i):
        return list(range(i * G + h * 256, i * G + (h + 1) * 256))
    def partner(cols):
        out = []
        for f, c in enumerate(cols):
            d = f % 64
            if d < 8:
                out.append(cols[f + 8])
            elif d < 16:
                out.append(cols[f - 8])
            else:
                out.append(c)
        return out
    a_u, a_g, q, kk, v, b_g, c_x, c_g, m_q, m_g = [seg(i) for i in range(10)]
    order = a_u + a_g + q + partner(q) + kk + partner(kk) + b_g + c_x + c_g + m_q + m_g
    return np.array(order, dtype=np.int64), np.array(v, dtype=np.int64)


def pack_weights(inp, depth, h):
    f32 = np.float32
    cols, vcols = _cols_in(h)
    WALL = np.zeros((depth, NCH, 128, 1024), f32)
    SP = np.zeros((depth, 128, NSP), f32)
    hs = slice(2 * h, 2 * h + 2)
    for l in range(depth):
        w_in = np.asarray(inp["w_in"][l], f32)
        wi = w_in[:, cols].reshape(8, 128, 22, 128)
        WALL[l, 0:22] = wi.transpose(2, 1, 0, 3).reshape(22, 128, 1024)
        wv = w_in[:, vcols].reshape(8, 128, 256)
        WALL[l, 22:24] = wv.reshape(2, 4, 128, 256).transpose(0, 2, 1, 3).reshape(2, 128, 1024)
        rows = np.concatenate([g * 512 + h * 256 + np.arange(256) for g in range(4)])
        wo = np.asarray(inp["w_out"][l], f32)[rows, :].reshape(8, 128, 8, 128)
        WALL[l, 24:32] = wo.transpose(2, 1, 0, 3).reshape(8, 128, 1024)
        gcols = np.concatenate([h * 256 + np.arange(256), 512 + h * 256 + np.arange(256)])
        wg = np.asarray(inp["s5_w_glu"][l], f32)[:, gcols]
        WALL[l, 32:34] = wg.reshape(2, 2, 128, 512).transpose(0, 2, 1, 3).reshape(2, 128, 1024)
        wm = np.asarray(inp["w_mem_kv"][l], f32)
        wk = wm[:, h * 256:(h + 1) * 256].reshape(8, 128, 2, 128)
        WALL[l, 34:36] = wk.transpose(2, 1, 0, 3).reshape(2, 128, 1024)
        wmv = wm[:, 512 + h * 256:512 + (h + 1) * 256].reshape(8, 128, 256)
        WALL[l, 36:38] = wmv.reshape(2, 4, 128, 256).transpose(0, 2, 1, 3).reshape(2, 128, 1024)
        bd = np.zeros((128, 2, 4, 128), f32)
        for wi_, nm in enumerate(("lru_w_a", "lru_w_x")):
            w = np.asarray(inp[nm][l], f32)
            for ct in range(2):
                for b2 in range(2):
                    bd[b2 * 64:(b2 + 1) * 64, wi_, ct, b2 * 64:(b2 + 1) * 64] = w[2 * (2 * h + ct) + b2]
        WALL[l, 38] = bd.reshape(128, 1024)
        bt = np.zeros((4, 2, 16, 2, 4, 2, 64), f32)
        ctt = np.zeros((2, 64, 2, 16, 2, 16), f32)
        for ri, (bn, cn) in enumerate((("s5_b_re", "s5_c_re"), ("s5_b_im", "s5_c_im"))):
            b = np.asarray(inp[bn][l], f32)
            c = np.asarray(inp[cn][l], f32)
            for st in range(8):
                ct, q = st // 4, st % 4
                for gl in range(2):
                    g = 2 * (8 * h + st) + gl
                    bt[q, gl, :, ri, ct, gl, :] = b[g].T
                    ctt[gl, :, ri, st, gl, :] = c[g].T
        WALL[l, 39] = bt.reshape(128, 1024)
        WALL[l, 40] = ctt.reshape(128, 1024)
        SP[l, :, 0:8] = np.asarray(inp["norm_g"][l], f32).reshape(8, 128).T
        SP[l, :, 8:16] = np.asarray(inp["mem_norm_g"][l], f32).reshape(8, 128).T
        lr = np.asarray(inp["s5_lambda_re"][l], f32).reshape(16, 2 * 64).T
        li = np.asarray(inp["s5_lambda_im"][l], f32).reshape(16, 2 * 64).T
        ld = np.repeat(np.asarray(inp["s5_log_dt"][l], f32).reshape(16, 2, 1), 64, axis=2).reshape(16, 128).T
        for rep in range(2):
            SP[l, :, 16 + 8 * rep:24 + 8 * rep] = lr[:, 8 * h:8 * h + 8]
            SP[l, :, 32 + 8 * rep:40 + 8 * rep] = li[:, 8 * h:8 * h + 8]
            SP[l, :, 48 + 8 * rep:56 + 8 * rep] = ld[:, 8 * h:8 * h + 8]
        SP[l, :, 64:66] = np.asarray(inp["s5_d"][l], f32).reshape(4, 128).T[:, hs]
        SP[l, :, 68] = np.asarray(inp["da_subln_g"][l], f32)
        cw = np.asarray(inp["lru_conv_w"][l], f32).reshape(4, 4, 128)
        SP[l, :, 69:77] = cw.transpose(2, 1, 0)[:, hs, :].reshape(128, 8)
        SP[l, :, 85:87] = np.asarray(inp["lru_conv_b"][l], f32).reshape(4, 128).T[:, hs]
        SP[l, :, 89:91] = np.asarray(inp["lru_b_a"][l], f32).reshape(4, 128).T[:, hs]
        SP[l, :, 93:95] = np.asarray(inp["lru_b_x"][l], f32).reshape(4, 128).T[:, hs]
        SP[l, :, 97:99] = np.asarray(inp["lru_lambda"][l], f32).reshape(4, 128).T[:, hs]
        for i, nm in enumerate(("da_lambda_q1", "da_lambda_k1", "da_lambda_q2", "da_lambda_k2")):
            SP[l, :, 101 + 64 * i:101 + 64 * (i + 1)] = np.asarray(inp[nm][l], f32)[None, :]
    CN = np.zeros((128, NCN), f32)
    CN[:, 0:129] = np.arange(129, dtype=f32)[None, :]
    for p in range(128):
        d = p % 64
        if d < 16:
            i = d % 8
            CN[p, 129] = 500000.0 ** (-(2.0 * i) / 16.0)
            CN[p, 130] = -1.0 if d < 8 else 1.0
        else:
            CN[p, 129] = 0.0
            CN[p, 130] = 1.0
    CN[:, 131] = EPS
    CN[:, 132] = 0.0
    CN[:, 133] = 1.0
    CN[:, 134:262] = (np.arange(128)[:, None] <= np.arange(128)[None, :]).astype(f32)
    CN[0, 262:390] = 1.0
    CN[32, 390:518] = 1.0
    FN = np.asarray(inp["final_norm_g"], f32).reshape(8, 128).T.copy()
    return WALL, SP, CN, FN


def build(L, depth, debug=False):
    NT = L // TT
    nc = bass.Bass("TRN2", target_bir_lowering=False)
    k = K(nc)
    xT = nc.dram_tensor("xT", [D, L], F32, kind="ExternalInput").ap()
    memT = nc.dram_tensor("memT", [D, 256], F32, kind="ExternalInput").ap()
    posd = nc.dram_tensor("pos", [1, L], I32, kind="ExternalInput").ap()
    walld = nc.dram_tensor("wall", [depth, NCH, 128, 1024], F32, kind="ExternalInput").ap()
    spd = nc.dram_tensor("sp", [depth, 128, NSP], F32, kind="ExternalInput").ap()
    cnd = nc.dram_tensor("cn", [128, NCN], F32, kind="ExternalInput").ap()
    fnd = nc.dram_tensor("fn", [128, 8], F32, kind="ExternalInput").ap()
    yT = nc.dram_tensor("yT", [D, L], F32, kind="ExternalOutput").ap()
    WB = k.dram("WB", [depth, NCH, 128, 1024], BF16)
    NC_ = NT // CH
    CW = CH * TT
    XRc = [k.dram(f"XR{c}", [D, CW], F32) for c in range(NC_)]
    PTc = [k.dram(f"PT{c}", [D, CW], F32) for c in range(NC_)]
    GYLc = [k.dram(f"GYL{c}", [256, CW], BF16) for c in range(NC_)]
    GYFc = [k.dram(f"GYF{c}", [512, CW], BF16) for c in range(NC_)]
    PJ2 = [k.dram(f"PJ{i}", [36, 128, L], BF16) for i in range(2)]
    VS2 = [k.dram(f"VS{i}", [L, 256], BF16) for i in range(2)]
    MX2 = [k.dram(f"MX{i}", [16, 128, L], BF16) for i in range(2)]
    WBb = [[Buf(f"WB{l}_{c}") for c in range(NCH)] for l in range(depth)]
    XRb = [[Buf(f"XR{kt}_{t}") for t in range(NT)] for kt in range(8)]
    PTb = [[Buf(f"PT{kt}_{t}") for t in range(NT)] for kt in range(8)]
    GYLb = [[Buf(f"GYL{ct}_{t}") for t in range(NT)] for ct in range(NH)]
    GYFb = [Buf(f"GYF{c}") for c in range(NC_)]
    PJb2 = [[[Buf(f"PJ{i}_{b}_{t}") for t in range(NT)] for b in range(36)] for i in range(2)]
    VSb2 = [[Buf(f"VS{i}_{t}") for t in range(NT)] for i in range(2)]
    MXb2 = [[[Buf(f"MX{i}_{b}_{t}") for t in range(NT)] for b in range(16)] for i in range(2)]
    YB = Buf("YB")
    xTv = xT.rearrange("(kt p) t -> p kt t", p=128)
    yTv = yT.rearrange("(kt p) t -> p kt t", p=128)
    def csl(tt):
        o = (tt % CH) * TT
        return tt // CH, slice(o, o + TT)
    def XRv(tt):
        c, s = csl(tt)
        return XRc[c].rearrange("(kt p) t -> p kt t", p=128)[:, :, s]

    def ring(name, n, shape, dt):
        bufs = [k.sb(f"{name}{i}", shape, dt) for i in range(n)]
        st = {"i": 0}
        def nxt():
            b = bufs[st["i"] % n]
            st["i"] += 1
            return b
        return nxt
    cn = k.sb("cn", [128, NCN], F32)
    spt2 = [k.sb(f"spt{i}", [128, NSP], F32) for i in range(2)]
    fnt = k.sb("fnt", [128, 8], F32)
    ones = k.sb("ones", [128, 128], BF16)
    ones32 = k.sb("ones32", [128, 128], F32)
    tri = k.sb("tri", [128, 128], BF16)
    xs = ring("xs", 1, [128, 8, TT], F32)
    sqb = k.sb("sqb", [128, 8, TT], BF16)
    hb = k.sb("hb", [128, 8, TT], BF16)
    wslot = ring("wsl", 4, [128, 1024], BF16)
    def ring_v(name, n, shape, dt, w):
        bufs = []
        for i in range(n):
            b = k.sb(f"{name}{i}", shape, dt)
            b.full = b.ap
            b.ap = b.ap[:, 0:w]
            bufs.append(b)
        st = {"i": 0}
        def nxt():
            b = bufs[st["i"] % n]
            st["i"] += 1
            return b
        return nxt
    t32 = ring_v("t32_", 8, [128, 516], F32, TT)
    tsb = t32
    t16 = ring("t16_", 4, [128, TT], BF16)
    xr16 = ring("xr16_", 4, [128, TT], BF16)
    ld16 = ring("ld16_", 3, [128, TT], BF16)
    ld16A = ring("ld16A_", 3, [128, TT], BF16)
    ub = ring("ub_", 2, [128, TT], BF16)
    st16 = ring("st16_", 4, [128, TT], BF16)
    st32 = ring("st32_", 2, [128, TT], F32)
    posi = k.sb("posi", [128, TT], I32)
    kint = k.sb("kint", [128, 516], I32)
    cosT = k.sb("cosT", [128, TT], F32)
    sinT = k.sb("sinT", [128, TT], F32)
    COS = k.sb("COS", [128, 16, 129], F32)
    SIN = k.sb("SIN", [128, 16, 129], F32)
    GRE = k.sb("GRE", [128, 16, 128], F32)
    GIM = k.sb("GIM", [128, 16, 128], F32)
    RPAT = k.sb("RPAT", [128, 16, 128], F32)
    s5p = k.sb("s5p", [128, 16, 16], F32)
    s5i = k.sb("s5i", [128, 2, 16], F32)
    s5tmp = ring("s5tmp", 4, [128, 1], F32)
    btb = k.sb("btb", [128, 1024], BF16)
    ctb = k.sb("ctb", [128, 1024], BF16)
    wglu = k.sb("wglu", [128, 4, 512], BF16)
    gyf = ring("gyf", 4, [128, TT], BF16)
    gy = [k.sb(f"gy{i}", [128, TT], BF16) for i in range(4)]
    wre = ring("wre", 2, [128, TT], F32)
    wim = ring("wim", 2, [128, TT], F32)
    lrub = k.sb("lrub", [128, 1024], BF16)
    cxb = [k.sb(f"cxb{i}", [128, 3 + TT], BF16) for i in range(4)]
    lcar = k.sb("lcar", [128, 4], F32)
    lsc = k.sb("lsc", [128, 4], F32)
    kmem = k.sb("kmem", [128, 4, 256], BF16)
    vmem = k.sb("vmem", [128, 2, 512], BF16)
    dap2 = [k.sb(f"dap{i}", [128, 8], F32) for i in range(2)]
    vblk = ring("vblk", 3, [128, 4, 128], BF16)
    qtr = ring("qtr", 2, [128, TT], BF16)
    fin_l = k.sb("fin_l", [128, TT], F32)
    fin_o = k.sb("fin_o", [128, TT], F32)
    fin_sq = k.sb("fin_sq", [128, TT], BF16)
    wo_s = ring("wo_s", 2, [128, 1024], BF16)
    mxs = [k.sb(f"mxs{i}", [128, TT], BF16) for i in range(16)]
    PS = [k.ps(f"ps{i}", [128, TT], F32) for i in range(8)]
    ROT = [PS[2], PS[3]]
    LS = PS[7]
    LSb = [Buf("LS0"), Buf("LS1")]

    psr = {"i": 0, "a": 0}
    def psn():
        b = ROT[psr["i"] % len(ROT)]
        psr["i"] += 1
        return b
    def psa():
        b = PS[psr["a"] % 2]
        psr["a"] += 1
        return b

    DV = "dve"
    AC = "act"
    PL = "pool"

    def cst(col):
        return cn[:, col:col + 1]

    def range_reduce_sin(out_ap, ang_ap, ki_ap, kf_ap, bufs_r, bufs_w, scale=1.0):
        k.op(DV, lambda e: e.tensor_scalar(out=ki_ap, in0=ang_ap, scalar1=float(1.0 / TWO_PI), scalar2=None, op0=ALU.mult), bufs_r, bufs_w["ki"])
        k.op(DV, lambda e: e.tensor_copy(out=kf_ap, in_=ki_ap), bufs_w["ki"], bufs_w["kf"])
        k.op(DV, lambda e: e.scalar_tensor_tensor(out=kf_ap, in0=kf_ap, scalar=-TWO_PI, in1=ang_ap, op0=ALU.mult, op1=ALU.add), bufs_w["kf"] + bufs_r, bufs_w["kf"])
        k.op(AC, lambda e: e.activation(out=out_ap, in_=kf_ap, func=AF.Sin, scale=scale), bufs_w["kf"], bufs_w["out"])

    def rms_rstd(ps_buf, ncols, dim):
        r = t32()
        k.op(AC, lambda e: e.activation(out=r[:, 0:ncols], in_=ps_buf[:, 0:ncols], func=AF.Sqrt, bias=cst(131), scale=float(1.0 / dim)), [ps_buf, cn], [r])
        k.op(DV, lambda e: e.reciprocal(out=r[:, 0:ncols], in_=r[:, 0:ncols]), [r], [r])
        return r

    def u_init():
        k.dma(cn[:], cnd, writes=[cn])
        k.dma(fnt[:], fnd, writes=[fnt])
        k.op(DV, lambda e: e.memset(ones[:], 1.0), [], [ones])
        k.op(DV, lambda e: e.memset(ones32[:], 1.0), [], [ones32])
        k.op(DV, lambda e: e.memset(fin_l[:], 0.0), [], [fin_l])
        k.op(DV, lambda e: e.tensor_copy(out=tri[:], in_=cn[:, 134:262]), [cn], [tri])
        yield

    def u_conv(l):
        ci = 0
        for c0 in range(0, NCH, 4):
            n = min(4, NCH - c0)
            xb_ = xs()
            xv = xb_[:].rearrange("p a t -> p (a t)").rearrange("p (c n) -> p c n", c=4)
            k.dma(xv[:, 0:n, :], walld[l, c0:c0 + n].rearrange("c p n -> p c n"), writes=[xb_])
            for j in range(n):
                wb = wslot()
                if ci % 2 == 0:
                    k.op(DV, lambda e, wb=wb, xv=xv, j=j: e.tensor_copy(out=wb[:], in_=xv[:, j, :]), [xb_], [wb])
                else:
                    k.op(AC, lambda e, wb=wb, xv=xv, j=j: e.activation(out=wb[:], in_=xv[:, j, :], func=AF.Copy), [xb_], [wb])
                k.dma(WB[l, c0 + j], wb[:], reads=[wb], writes=[WBb[l][c0 + j]], q="pool")
                ci += 1
            yield

    def u_setupA(l):
        S = spt2[l % 2]
        k.dma(S[:], spd[l], writes=[S])
        yield

    def u_s1m(l):
        S = spt2[l % 2]
        memxb = xs()
        memx = memxb[:, :, 0:256]
        memn = hb[:, :, 0:256]
        k.dma(memx, memT.rearrange("(kt p) m -> p kt m", p=128), writes=[memxb])
        pm = psn()
        for kt in range(8):
            k.op(AC, lambda e, kt=kt: e.activation(out=sqb[:, kt, 0:256], in_=memx[:, kt, :], func=AF.Square), [memxb], [sqb])
        for kt in range(8):
            k.op("pe", lambda e, kt=kt, pm=pm: e.matmul(pm[:, 0:256], lhsT=ones[:], rhs=sqb[:, kt, 0:256], start=(kt == 0), stop=(kt == 7)),
                 [ones, sqb], [pm], inc=(kt == 7))
        rm = rms_rstd(pm, 256, D)
        for kt in range(8):
            k.op(DV, lambda e, kt=kt, rm=rm: e.scalar_tensor_tensor(out=memn[:, kt, :], in0=memx[:, kt, :], scalar=S[:, 8 + kt:9 + kt], in1=rm[:, 0:256],
                                                                      op0=ALU.mult, op1=ALU.mult), [memxb, S, rm], [hb])
        for hd in range(NH):
            w = wslot()
            k.dma(w[:], WB[l, 34 + hd], reads=[WBb[l][34 + hd]], writes=[w])
            pk = psn()
            for kt in range(8):
                k.op("pe", lambda e, kt=kt, w=w, pk=pk: e.matmul(pk[:, 0:256], lhsT=w[:, kt * 128:(kt + 1) * 128], rhs=memn[:, kt, :], start=(kt == 0), stop=(kt == 7)),
                     [w, hb], [pk], inc=(kt == 7))
            k.op(AC, lambda e, hd=hd, pk=pk: e.activation(out=kmem[:, hd, :], in_=pk[:, 0:256], func=AF.Copy), [pk], [kmem])
        wv4 = [wslot() for _ in range(2)]
        for j in range(2):
            k.dma(wv4[j][:], WB[l, 36 + j], reads=[WBb[l][36 + j]], writes=[wv4[j]])
        for mt in range(2):
            pv = psn()
            for kt in range(8):
                w = wv4[kt // 4]
                k.op("pe", lambda e, kt=kt, w=w, pv=pv, mt=mt: e.matmul(pv[:, 0:256], lhsT=memn[:, kt, mt * 128:(mt + 1) * 128], rhs=w[:, (kt % 4) * 256:(kt % 4 + 1) * 256],
                                                                        start=(kt == 0), stop=(kt == 7)), [w, hb], [pv], inc=(kt == 7))
            k.op(AC, lambda e, mt=mt, pv=pv: e.activation(out=vmem[:, mt, 0:256], in_=pv[:, 0:256], func=AF.Copy), [pv], [vmem])
        yield

    def u_setupB(l):
        S = spt2[l % 2]
        dap = dap2[l % 2]
        lam_init = 0.8 - 0.6 * math.exp(-0.3 * l)
        tq = t32()
        k.op(DV, lambda e: e.tensor_tensor(out=tq[:, 0:64], in0=S[:, 101:165], in1=S[:, 165:229], op=ALU.mult), [S], [tq])
        k.op(DV, lambda e: e.reduce_sum(out=dap[:, 0:1], in_=tq[:, 0:64], axis=AX.X), [tq], [dap])
        k.op(DV, lambda e: e.tensor_tensor(out=tq[:, 64:128], in0=S[:, 229:293], in1=S[:, 293:357], op=ALU.mult), [S], [tq])
        k.op(DV, lambda e: e.reduce_sum(out=dap[:, 1:2], in_=tq[:, 64:128], axis=AX.X), [tq], [dap])
        k.op(AC, lambda e: e.activation(out=dap[:, 2:4], in_=dap[:, 0:2], func=AF.Exp), [dap], [dap])
        k.op(DV, lambda e: e.tensor_tensor(out=dap[:, 4:5], in0=dap[:, 3:4], in1=dap[:, 2:3], op=ALU.subtract), [dap], [dap])
        k.op(DV, lambda e: e.tensor_scalar(out=dap[:, 4:5], in0=dap[:, 4:5], scalar1=float(-lam_init), scalar2=None, op0=ALU.add), [dap], [dap])
        k.op(DV, lambda e: e.tensor_scalar(out=dap[:, 5:6], in0=S[:, 68:69], scalar1=float(1.0 - lam_init), scalar2=None, op0=ALU.mult), [S], [dap])
        k.op(AC, lambda e: e.activation(out=lsc[:], in_=S[:, 97:101], func=AF.Exp, scale=-1.0), [S], [lsc])
        k.op(AC, lambda e: e.activation(out=lsc[:], in_=lsc[:], func=AF.Ln, bias=cst(133), scale=1.0), [lsc, cn], [lsc])
        k.op(DV, lambda e: e.tensor_scalar(out=lsc[:], in0=lsc[:], scalar1=-8.0, scalar2=None, op0=ALU.mult), [lsc], [lsc])
        k.op(DV, lambda e: e.memset(lcar[:], 0.0), [], [lcar])
        for ct in range(NH):
            k.op(DV, lambda e, ct=ct: e.memset(cxb[ct][:, 0:3], 0.0), [], [cxb[ct]])
        k.dma(lrub[:], WB[l, 38], reads=[WBb[l][38]], writes=[lrub])
        LR = S[:, 16:32]; LI = S[:, 32:48]; LD = S[:, 48:64]
        P = lambda i: s5p[:, i, :]
        k.op(AC, lambda e: e.activation(out=P(0), in_=LD, func=AF.Exp), [S], [s5p])
        k.op(DV, lambda e: e.tensor_tensor(out=P(8), in0=LR, in1=P(0), op=ALU.mult), [S, s5p], [s5p])
        k.op(AC, lambda e: e.activation(out=P(1), in_=P(8), func=AF.Exp), [s5p], [s5p])
        k.op(DV, lambda e: e.tensor_tensor(out=P(2), in0=LI, in1=P(0), op=ALU.mult), [S, s5p], [s5p])
        for g4 in range(2):
            ss = slice(4 * g4, 4 * g4 + 4)
            b1 = tsb(); b2 = tsb()
            b1v3 = b1.full[:, 0:516].rearrange("p (s j) -> p s j", s=4)
            b2v3 = b2.full[:, 0:516].rearrange("p (s j) -> p s j", s=4)
            kiv = kint[:, 0:516].rearrange("p (s j) -> p s j", s=4)
            k.op(DV, lambda e, b1v3=b1v3, ss=ss: e.tensor_tensor(out=b1v3, in0=s5p[:, 2, ss].unsqueeze(2).to_broadcast([128, 4, 129]),
                                               in1=cn[:, 0:129].unsqueeze(1).to_broadcast([128, 4, 129]), op=ALU.mult), [s5p, cn], [b1])
            range_reduce_sin(SIN[:, ss, :], b1v3, kiv, b2v3, [b1], {"ki": [kint], "kf": [b2], "out": [SIN]})
            k.op(DV, lambda e, b1v3=b1v3: e.tensor_scalar(out=b1v3, in0=b1v3, scalar1=float(np.pi / 2), scalar2=None, op0=ALU.add), [b1], [b1])
            range_reduce_sin(COS[:, ss, :], b1v3, kiv, b2v3, [b1], {"ki": [kint], "kf": [b2], "out": [COS]})
        k.op(DV, lambda e: e.tensor_tensor(out=P(3)[:, 0:8], in0=P(1)[:, 0:8], in1=COS[:, 0:8, 1], op=ALU.mult), [s5p, COS], [s5p])
        k.op(DV, lambda e: e.tensor_scalar(out=P(3), in0=P(3), scalar1=-1.0, scalar2=None, op0=ALU.add), [s5p], [s5p])
        k.op(DV, lambda e: e.tensor_tensor(out=P(4)[:, 0:8], in0=P(1)[:, 0:8], in1=SIN[:, 0:8, 1], op=ALU.mult), [s5p, SIN], [s5p])
        k.op(DV, lambda e: e.tensor_tensor(out=P(5), in0=LR, in1=LR, op=ALU.mult), [S], [s5p])
        k.op(DV, lambda e: e.tensor_tensor(out=P(8), in0=LI, in1=LI, op=ALU.mult), [S], [s5p])
        k.op(DV, lambda e: e.tensor_tensor(out=P(5), in0=P(5), in1=P(8), op=ALU.add), [s5p], [s5p])
        k.op(DV, lambda e: e.reciprocal(out=P(5), in_=P(5)), [s5p], [s5p])
        k.op(DV, lambda e: e.tensor_tensor(out=P(8), in0=P(3), in1=LR, op=ALU.mult), [s5p, S], [s5p])
        k.op(DV, lambda e: e.tensor_tensor(out=P(9), in0=P(4), in1=LI, op=ALU.mult), [s5p, S], [s5p])
        k.op(DV, lambda e: e.tensor_tensor(out=P(8), in0=P(8), in1=P(9), op=ALU.add), [s5p], [s5p])
        k.op(DV, lambda e: e.tensor_tensor(out=P(6), in0=P(8), in1=P(5), op=ALU.mult), [s5p], [s5p])
        k.op(DV, lambda e: e.tensor_tensor(out=P(8), in0=P(4), in1=LR, op=ALU.mult), [s5p, S], [s5p])
        k.op(DV, lambda e: e.tensor_tensor(out=P(9), in0=P(3), in1=LI, op=ALU.mult), [s5p, S], [s5p])
        k.op(DV, lambda e: e.tensor_tensor(out=P(8), in0=P(8), in1=P(9), op=ALU.subtract), [s5p], [s5p])
        k.op(DV, lambda e: e.tensor_tensor(out=P(7), in0=P(8), in1=P(5), op=ALU.mult), [s5p], [s5p])
        for g4 in range(2):
            ss = slice(4 * g4, 4 * g4 + 4)
            bc = lambda i, ss=ss: s5p[:, i, ss].unsqueeze(2).to_broadcast([128, 4, 128])
            b1 = tsb(); b2 = tsb()
            b1v = b1[:, 0:512].rearrange("p (s j) -> p s j", s=4)
            b2v = b2[:, 0:512].rearrange("p (s j) -> p s j", s=4)
            C128 = COS[:, ss, 0:128]; S128 = SIN[:, ss, 0:128]
            k.op(DV, lambda e, b1v=b1v, C128=C128, bc=bc: e.tensor_tensor(out=b1v, in0=C128, in1=bc(6), op=ALU.mult), [COS, s5p], [b1])
            k.op(DV, lambda e, b2v=b2v, S128=S128, bc=bc: e.tensor_tensor(out=b2v, in0=S128, in1=bc(7), op=ALU.mult), [SIN, s5p], [b2])
            k.op(DV, lambda e, b1v=b1v, b2v=b2v, ss=ss: e.tensor_tensor(out=GRE[:, ss, :], in0=b1v, in1=b2v, op=ALU.add), [b1, b2], [GRE])
            k.op(DV, lambda e, b1v=b1v, C128=C128, bc=bc: e.tensor_tensor(out=b1v, in0=C128, in1=bc(7), op=ALU.mult), [COS, s5p], [b1])
            k.op(DV, lambda e, b2v=b2v, S128=S128, bc=bc: e.tensor_tensor(out=b2v, in0=S128, in1=bc(6), op=ALU.mult), [SIN, s5p], [b2])
            k.op(DV, lambda e, b1v=b1v, b2v=b2v, ss=ss: e.tensor_tensor(out=GIM[:, ss, :], in0=b1v, in1=b2v, op=ALU.subtract), [b1, b2], [GIM])
            k.op(DV, lambda e, ss=ss, bc=bc: e.tensor_copy(out=RPAT[:, ss, :], in_=bc(1)), [s5p], [RPAT])
        k.op(DV, lambda e: e.memset(s5i[:], 0.0), [], [s5i])
        k.dma(btb[:], WB[l, 39], reads=[WBb[l][39]], writes=[btb])
        k.dma(ctb[:], WB[l, 40], reads=[WBb[l][40]], writes=[ctb])
        k.op(DV, lambda e: e.tensor_scalar(out=ctb[:, 512:1024], in0=ctb[:, 512:1024], scalar1=-1.0, scalar2=None, op0=ALU.mult), [ctb], [ctb])
        for c2 in range(2):
            k.dma(wglu[:, 2 * c2:2 * c2 + 2, :], WB[l, 32 + c2].rearrange("p (a n) -> p a n", a=2), reads=[WBb[l][32 + c2]], writes=[wglu])
        yield

    btv = btb[:].rearrange("p (r c m) -> p r c m", r=2, c=4)
    ctv = ctb[:].rearrange("p (r s m) -> p r s m", r=2, s=16)
    v4 = lambda ap: ap.rearrange("p (c j) -> p c j", c=4)

    def u_s1(l, tt):
        S = spt2[l % 2]
        PJ = PJ2[l % 2]; PJb = PJb2[l % 2]; VS = VS2[l % 2]; VSb = VSb2[l % 2]
        sl = slice(tt * TT, (tt + 1) * TT)
        x_ = xs()
        if l == 0:
            k.dma(x_[:], xTv[:, :, sl], writes=[x_])
        else:
            k.dma(x_[:], XRv(tt), reads=[XRb[kt][tt] for kt in range(8)], writes=[x_])
        for kt in range(8):
            k.op(AC, lambda e, kt=kt: e.activation(out=sqb[:, kt, :], in_=x_[:, kt, :], func=AF.Square), [x_], [sqb])
        pn = psn()
        for kt in range(8):
            k.op("pe", lambda e, kt=kt: e.matmul(pn[:], lhsT=ones[:], rhs=sqb[:, kt, :], start=(kt == 0), stop=(kt == 7)), [ones, sqb], [pn], inc=(kt == 7))
        rr = rms_rstd(pn, TT, D)
        for kt in range(8):
            k.op(DV, lambda e, kt=kt: e.scalar_tensor_tensor(out=hb[:, kt, :], in0=x_[:, kt, :], scalar=S[:, kt:kt + 1], in1=rr[:],
                                                            op0=ALU.mult, op1=ALU.mult), [x_, S, rr], [hb])
        k.dma(posi[:], posd[0:1, sl].to_broadcast([128, TT]), writes=[posi])
        ang = t32(); kf = t32()
        k.op(DV, lambda e: e.tensor_copy(out=ang[:], in_=posi[:]), [posi], [ang])
        k.op(DV, lambda e: e.tensor_scalar(out=ang[:], in0=ang[:], scalar1=cst(129), scalar2=None, op0=ALU.mult), [ang, cn], [ang])
        kiv2 = kint[:, 0:TT]
        range_reduce_sin(sinT[:], ang[:], kiv2, kf[:], [ang], {"ki": [kint], "kf": [kf], "out": [sinT]}, scale=cst(130))
        k.op(DV, lambda e: e.tensor_scalar(out=ang[:], in0=ang[:], scalar1=float(np.pi / 2), scalar2=None, op0=ALU.add), [ang], [ang])
        range_reduce_sin(cosT[:], ang[:], kiv2, kf[:], [ang], {"ki": [kint], "kf": [kf], "out": [cosT]})
        yield

        plain = [(0, 0), (1, 1), (2, 4), (3, 5)] + [(12 + 2 * g + i, 16 + 4 * g + i) for g in range(5) for i in range(2)]
        order = [b for (b, _) in plain]
        for base in (4, 8):
            for hd in range(NH):
                order += [base + hd, base + 2 + hd]
        wq = {}
        nxt_ld = {"i": 0}
        def prefetch(upto):
            while nxt_ld["i"] < min(upto, len(order)):
                b = order[nxt_ld["i"]]
                w = wslot()
                k.dma(w[:], WB[l, b], reads=[WBb[l][b]], writes=[w])
                wq[nxt_ld["i"]] = w
                nxt_ld["i"] += 1
        pos_ = {"i": 0}
        def proj_block(blk):
            i = pos_["i"]
            assert order[i] == blk
            prefetch(i + 3)
            w = wq.pop(i)
            pos_["i"] += 1
            pp = psn()
            for kt in range(8):
                k.op("pe", lambda e, kt=kt, w=w, pp=pp: e.matmul(pp[:], lhsT=w[:, kt * 128:(kt + 1) * 128], rhs=hb[:, kt, :], start=(kt == 0), stop=(kt == 7)),
                     [w, hb], [pp], inc=(kt == 7))
            return pp
        ev = 0
        for (blk, slot) in plain:
            pp = proj_block(blk)
            o = st16()
            if ev % 2 == 0:
                k.op(AC, lambda e, o=o, pp=pp: e.activation(out=o[:], in_=pp[:], func=AF.Copy), [pp], [o])
            else:
                k.op(DV, lambda e, o=o, pp=pp: e.tensor_copy(out=o[:], in_=pp[:]), [pp], [o])
            ev += 1
            k.dma(PJ[slot][:, sl], o[:], reads=[o], writes=[PJb[slot][tt]], q="pool")
            if ev % 2 == 0:
                yield
        for base, slot0 in ((4, 8), (8, 12)):
            for hd in range(NH):
                p1 = proj_block(base + hd)
                p2 = proj_block(base + 2 + hd)
                a1 = t32(); a2 = t32(); o = st16()
                k.op(DV, lambda e, a1=a1, p1=p1: e.tensor_tensor(out=a1[:], in0=p1[:], in1=cosT[:], op=ALU.mult), [p1, cosT], [a1])
                k.op(DV, lambda e, a2=a2, p2=p2: e.tensor_tensor(out=a2[:], in0=p2[:], in1=sinT[:], op=ALU.mult), [p2, sinT], [a2])
                k.op(DV, lambda e, a1=a1, a2=a2, o=o: e.tensor_tensor(out=o[:], in0=a1[:], in1=a2[:], op=ALU.add), [a1, a2], [o])
                k.dma(PJ[slot0 + hd][:, sl], o[:], reads=[o], writes=[PJb[slot0 + hd][tt]], q="pool")
                yield
        wv = [wslot() for _ in range(2)]
        for j in range(2):
            k.dma(wv[j][:], WB[l, 22 + j], reads=[WBb[l][22 + j]], writes=[wv[j]])
        for sub in range(4):
            pv = psn()
            for kt in range(8):
                w = wv[kt // 4]
                k.op("pe", lambda e, kt=kt, w=w, pv=pv, sub=sub: e.matmul(pv[:, 0:256], lhsT=hb[:, kt, sub * 128:(sub + 1) * 128], rhs=w[:, (kt % 4) * 256:(kt % 4 + 1) * 256],
                                                                          start=(kt == 0), stop=(kt == 7)), [w, hb], [pv], inc=(kt == 7))
            o = st16()
            k.op(AC, lambda e, o=o, pv=pv: e.activation(out=o[:, 0:256], in_=pv[:, 0:256], func=AF.Copy), [pv], [o])
            k.dma(VS[tt * TT + sub * 128: tt * TT + (sub + 1) * 128, :], o[:, 0:256], reads=[o], writes=[VSb[tt]], q="pool")
        yield

    def gate_store(l, ysrc_buf, ysrc_ap, gslot, mslot, tt, sl):
        PJ = PJ2[l % 2]; PJb = PJb2[l % 2]; MX = MX2[l % 2]; MXb = MXb2[l % 2]
        g = ld16()
        k.dma(g[:], PJ[gslot][:, sl], reads=[PJb[gslot][tt]], writes=[g])
        sg = t32()
        k.op(AC, lambda e: e.activation(out=sg[:], in_=g[:], func=AF.Silu), [g], [sg])
        o = st16()
        k.op(DV, lambda e: e.tensor_tensor(out=o[:], in0=ysrc_ap, in1=sg[:], op=ALU.mult), [ysrc_buf, sg], [o])
        k.dma(MX[mslot][:, sl], o[:], reads=[o], writes=[MXb[mslot][tt]], q="pool")

    def u_s2b(l, tt):
        S = spt2[l % 2]
        PJ = PJ2[l % 2]; PJb = PJb2[l % 2]
        sl = slice(tt * TT, (tt + 1) * TT)
        us = {}
        def load_u(ct):
            u = ub()
            k.dma(u[:], PJ[ct][:, sl], reads=[PJb[ct][tt]], writes=[u])
            us[ct] = u
        load_u(0)
        cchunk, csl_ = csl(tt)
        for ct in range(NH):
            u = us[ct]
            if ct + 1 < NH:
                load_u(ct + 1)
            yps = PS[6]
            pend = None
            for q in range(4):
                st = 4 * ct + q
                pre = psn(); pim = psn()
                k.op("pe", lambda e, pre=pre, u=u, q=q, ct=ct: e.matmul(pre[:], lhsT=btv[32 * q:32 * q + 32, 0, ct, :], rhs=u[32 * q:32 * q + 32, :], start=True, stop=True, tile_position=(32 * q, 0)), [btb, u], [pre])
                k.op("pe", lambda e, pim=pim, u=u, q=q, ct=ct: e.matmul(pim[:], lhsT=btv[32 * q:32 * q + 32, 1, ct, :], rhs=u[32 * q:32 * q + 32, :], start=True, stop=True, tile_position=(32 * q, 0)), [btb, u], [pim])
                bre = t32(); bim = t32()
                k.op(AC, lambda e, bre=bre, pre=pre: e.activation(out=bre[:], in_=pre[:], func=AF.Copy), [pre], [bre])
                k.op(AC, lambda e, bim=bim, pim=pim: e.activation(out=bim[:], in_=pim[:], func=AF.Copy), [pim], [bim])
                gre = GRE[:, st, :].unsqueeze(1).to_broadcast([128, 4, 128])
                gim = GIM[:, st, :].unsqueeze(1).to_broadcast([128, 4, 128])
                a1 = t32(); a2 = t32(); mre = t32(); mim = t32()
                k.op(DV, lambda e, a1=a1, bre=bre, gre=gre: e.tensor_tensor(out=v4(a1[:]), in0=v4(bre[:]), in1=gre, op=ALU.mult), [bre, GRE], [a1])
                k.op(DV, lambda e, a2=a2, bim=bim, gim=gim: e.tensor_tensor(out=v4(a2[:]), in0=v4(bim[:]), in1=gim, op=ALU.mult), [bim, GIM], [a2])
                k.op(DV, lambda e, a1=a1, a2=a2, mre=mre: e.tensor_tensor(out=mre[:], in0=a1[:], in1=a2[:], op=ALU.subtract), [a1, a2], [mre])
                a3 = t32(); a4 = t32()
                k.op(DV, lambda e, a3=a3, bim=bim, gre=gre: e.tensor_tensor(out=v4(a3[:]), in0=v4(bim[:]), in1=gre, op=ALU.mult), [bim, GRE], [a3])
                k.op(DV, lambda e, a4=a4, bre=bre, gim=gim: e.tensor_tensor(out=v4(a4[:]), in0=v4(bre[:]), in1=gim, op=ALU.mult), [bre, GIM], [a4])
                k.op(DV, lambda e, a3=a3, a4=a4, mim=mim: e.tensor_tensor(out=mim[:], in0=a3[:], in1=a4[:], op=ALU.add), [a3, a4], [mim])
                wr = wre(); wi = wim()
                for c in range(4):
                    cs = slice(c * 128, (c + 1) * 128)
                    k.op(DV, lambda e, wr=wr, mre=mre, cs=cs, st=st: e.tensor_tensor_scan(out=wr[:, cs], data0=RPAT[:, st, :], data1=mre[:, cs], initial=s5i[:, 0, st:st + 1],
                                                                                      op0=ALU.mult, op1=ALU.add), [RPAT, mre, s5i], [wr])
                    k.op(DV, lambda e, wi=wi, mim=mim, cs=cs, st=st: e.tensor_tensor_scan(out=wi[:, cs], data0=RPAT[:, st, :], data1=mim[:, cs], initial=s5i[:, 1, st:st + 1],
                                                                                      op0=ALU.mult, op1=ALU.add), [RPAT, mim, s5i], [wi])
                    e0 = c * 128 + 127
                    tA = s5tmp(); tB = s5tmp()
                    c128 = COS[:, st, 128:129]; s128 = SIN[:, st, 128:129]
                    k.op(DV, lambda e, tA=tA, wi=wi, e0=e0, s128=s128: e.tensor_scalar(out=tA[:], in0=wi[:, e0:e0 + 1], scalar1=s128, scalar2=None, op0=ALU.mult), [wi, SIN], [tA])
                    k.op(DV, lambda e, tB=tB, wr=wr, e0=e0, s128=s128: e.tensor_scalar(out=tB[:], in0=wr[:, e0:e0 + 1], scalar1=s128, scalar2=None, op0=ALU.mult), [wr, SIN], [tB])
                    k.op(DV, lambda e, tA=tA, wr=wr, e0=e0, c128=c128, st=st: e.scalar_tensor_tensor(out=s5i[:, 0, st:st + 1], in0=wr[:, e0:e0 + 1], scalar=c128, in1=tA[:],
                                                                                                op0=ALU.mult, op1=ALU.subtract), [wr, COS, tA], [s5i])
                    k.op(DV, lambda e, tB=tB, wi=wi, e0=e0, c128=c128, st=st: e.scalar_tensor_tensor(out=s5i[:, 1, st:st + 1], in0=wi[:, e0:e0 + 1], scalar=c128, in1=tB[:],
                                                                                                op0=ALU.mult, op1=ALU.add), [wi, COS, tB], [s5i])
                cosb = COS[:, st, 0:128].unsqueeze(1).to_broadcast([128, 4, 128])
                sinb = SIN[:, st, 0:128].unsqueeze(1).to_broadcast([128, 4, 128])
                d1 = t32(); d2 = t32(); xre = xr16(); nxi = xr16()
                DM = PL
                k.op(DM, lambda e, d1=d1, wr=wr, cosb=cosb: e.tensor_tensor(out=v4(d1[:]), in0=v4(wr[:]), in1=cosb, op=ALU.mult), [wr, COS], [d1])
                k.op(DM, lambda e, d2=d2, wi=wi, sinb=sinb: e.tensor_tensor(out=v4(d2[:]), in0=v4(wi[:]), in1=sinb, op=ALU.mult), [wi, SIN], [d2])
                k.op(DM, lambda e, d1=d1, d2=d2, xre=xre: e.tensor_tensor(out=xre[:], in0=d1[:], in1=d2[:], op=ALU.subtract), [d1, d2], [xre])
                d3 = t32(); d4 = t32()
                k.op(DM, lambda e, d3=d3, wi=wi, cosb=cosb: e.tensor_tensor(out=v4(d3[:]), in0=v4(wi[:]), in1=cosb, op=ALU.mult), [wi, COS], [d3])
                k.op(DM, lambda e, d4=d4, wr=wr, sinb=sinb: e.tensor_tensor(out=v4(d4[:]), in0=v4(wr[:]), in1=sinb, op=ALU.mult), [wr, SIN], [d4])
                k.op(DM, lambda e, d3=d3, d4=d4, nxi=nxi: e.tensor_tensor(out=nxi[:], in0=d3[:], in1=d4[:], op=ALU.add), [d3, d4], [nxi])
                if pend is not None:
                    pend()
                def mk(yps=yps, xre=xre, nxi=nxi, q=q, st=st):
                    def f():
                        k.op("pe", lambda e: e.matmul(yps[32 * q:32 * q + 32, :], lhsT=ctv[:, 0, st, :], rhs=xre[:], start=True, stop=False, tile_position=(0, 32 * q)), [ctb, xre], [yps], inc=False)
                        k.op("pe", lambda e: e.matmul(yps[32 * q:32 * q + 32, :], lhsT=ctv[:, 1, st, :], rhs=nxi[:], start=False, stop=True, tile_position=(0, 32 * q)), [ctb, nxi], [yps])
                    return f
                pend = mk()
                yield
            pend()
            yy = t32(); y2 = t32(); sg = t32()
            k.op(DV, lambda e, yy=yy, u=u, yps=yps, ct=ct: e.scalar_tensor_tensor(out=yy[:], in0=u[:], scalar=S[:, 64 + ct:65 + ct], in1=yps[:], op0=ALU.mult, op1=ALU.add), [u, S, yps], [yy])
            k.op(DV, lambda e, yy=yy, y2=y2: e.tensor_tensor(out=y2[:], in0=yy[:], in1=yy[:], op=ALU.mult), [yy], [y2])
            k.op(DV, lambda e, y2=y2: e.tensor_scalar(out=y2[:], in0=y2[:], scalar1=0.044715, scalar2=1.0, op0=ALU.mult, op1=ALU.add), [y2], [y2])
            k.op(DV, lambda e, yy=yy, y2=y2: e.tensor_tensor(out=y2[:], in0=y2[:], in1=yy[:], op=ALU.mult), [yy, y2], [y2])
            k.op(AC, lambda e, sg=sg, y2=y2: e.activation(out=sg[:], in_=y2[:], func=AF.Sigmoid, scale=float(2.0 * math.sqrt(2.0 / math.pi))), [y2], [sg])
            gq = gy[2 * (tt % 2) + ct]
            k.op(DV, lambda e, yy=yy, sg=sg, gq=gq: e.tensor_tensor(out=gq[:], in0=yy[:], in1=sg[:], op=ALU.mult), [yy, sg], [gq])
            k.dma(GYLc[cchunk][ct * 128:(ct + 1) * 128, csl_], gq[:], reads=[gq], writes=[GYLb[ct][tt]], q="pool")
            yield
        for ct in range(NH):
            cb = cxb[ct]
            k.dma(cb[:, 3:3 + TT], PJ[20 + ct][:, sl], reads=[PJb[20 + ct][tt]], writes=[cb])
            xc = t32(); xcb = t16()
            k.op(DV, lambda e, xc=xc, cb=cb, ct=ct: e.tensor_scalar(out=xc[:], in0=cb[:, 0:TT], scalar1=S[:, 69 + ct * 4:70 + ct * 4], scalar2=S[:, 85 + ct:86 + ct], op0=ALU.mult, op1=ALU.add), [cb, S], [xc])
            for kk in range(1, 4):
                k.op(DV, lambda e, xc=xc, cb=cb, ct=ct, kk=kk: e.scalar_tensor_tensor(out=xc[:], in0=cb[:, kk:kk + TT], scalar=S[:, 69 + ct * 4 + kk:70 + ct * 4 + kk], in1=xc[:], op0=ALU.mult, op1=ALU.add), [cb, S, xc], [xc])
            k.op(DV, lambda e, cb=cb: e.tensor_copy(out=cb[:, 0:3], in_=cb[:, TT:TT + 3]), [cb], [cb])
            k.op(AC, lambda e, xc=xc, xcb=xcb: e.activation(out=xcb[:], in_=xc[:], func=AF.Copy), [xc], [xcb])
            pr = psn(); pi_ = psn()
            k.op("pe", lambda e, pr=pr, xcb=xcb, ct=ct: e.matmul(pr[:], lhsT=lrub[:, ct * 128:(ct + 1) * 128], rhs=xcb[:], start=True, stop=True), [lrub, xcb], [pr])
            k.op("pe", lambda e, pi_=pi_, xcb=xcb, ct=ct: e.matmul(pi_[:], lhsT=lrub[:, 512 + ct * 128:512 + (ct + 1) * 128], rhs=xcb[:], start=True, stop=True), [lrub, xcb], [pi_])
            rg = t32(); ig = t32(); aa = t32(); om = t32()
            k.op(AC, lambda e, rg=rg, pr=pr, ct=ct: e.activation(out=rg[:], in_=pr[:], func=AF.Sigmoid, bias=S[:, 89 + ct:90 + ct], scale=1.0), [pr, S], [rg])
            k.op(AC, lambda e, ig=ig, pi_=pi_, ct=ct: e.activation(out=ig[:], in_=pi_[:], func=AF.Sigmoid, bias=S[:, 93 + ct:94 + ct], scale=1.0), [pi_, S], [ig])
            k.op(AC, lambda e, aa=aa, rg=rg, ct=ct: e.activation(out=aa[:], in_=rg[:], func=AF.Exp, scale=lsc[:, ct:ct + 1]), [rg, lsc], [aa])
            k.op(DV, lambda e, aa=aa, om=om: e.tensor_tensor(out=om[:], in0=aa[:], in1=aa[:], op=ALU.mult), [aa], [om])
            k.op(DV, lambda e, om=om: e.tensor_scalar(out=om[:], in0=om[:], scalar1=-1.0, scalar2=1.0, op0=ALU.mult, op1=ALU.add), [om], [om])
            k.op(AC, lambda e, om=om: e.activation(out=om[:], in_=om[:], func=AF.Sqrt), [om], [om])
            k.op(DV, lambda e, ig=ig, xc=xc: e.tensor_tensor(out=ig[:], in0=ig[:], in1=xc[:], op=ALU.mult), [ig, xc], [ig])
            k.op(DV, lambda e, ig=ig, om=om: e.tensor_tensor(out=ig[:], in0=ig[:], in1=om[:], op=ALU.mult), [ig, om], [ig])
            hh = t32()
            k.op(DV, lambda e, hh=hh, aa=aa, ig=ig, ct=ct: e.tensor_tensor_scan(out=hh[:], data0=aa[:], data1=ig[:], initial=lcar[:, ct:ct + 1], op0=ALU.mult, op1=ALU.add), [aa, ig, lcar], [hh])
            k.op(DV, lambda e, hh=hh, ct=ct: e.tensor_copy(out=lcar[:, ct:ct + 1], in_=hh[:, TT - 1:TT]), [hh], [lcar])
            gate_store(l, hh, hh[:], 24 + ct, 8 + ct, tt, sl)
            yield
        for hd in range(NH):
            mq = ld16()
            k.dma(mq[:], PJ[28 + hd][:, sl], reads=[PJb[28 + hd][tt]], writes=[mq])
            pes = []
            for mt in range(2):
                pss = psn()
                k.op("pe", lambda e, pss=pss, mq=mq, hd=hd, mt=mt: e.matmul(pss[:], lhsT=kmem[:, hd, mt * 128:(mt + 1) * 128], rhs=mq[:], start=True, stop=True), [kmem, mq], [pss])
                pe_ = t16()
                k.op(AC, lambda e, pe_=pe_, pss=pss: e.activation(out=pe_[:], in_=pss[:], func=AF.Exp, scale=float(128 ** -0.5)), [pss], [pe_])
                pes.append(pe_)
            po = psn(); pl = psn()
            for mt in range(2):
                k.op("pe", lambda e, po=po, pe_=pes[mt], hd=hd, mt=mt: e.matmul(po[:], lhsT=vmem[:, mt, hd * 128:(hd + 1) * 128], rhs=pe_[:], start=(mt == 0), stop=(mt == 1)), [vmem, pes[mt]], [po], inc=(mt == 1))
            for mt in range(2):
                k.op("pe", lambda e, pl=pl, pe_=pes[mt], mt=mt: e.matmul(pl[:], lhsT=ones[:], rhs=pe_[:], start=(mt == 0), stop=(mt == 1)), [ones, pes[mt]], [pl], inc=(mt == 1))
            rl = t32(); ym = t32()
            k.op(DV, lambda e, rl=rl, pl=pl: e.reciprocal(out=rl[:], in_=pl[:]), [pl], [rl])
            k.op(DV, lambda e, ym=ym, po=po, rl=rl: e.tensor_tensor(out=ym[:], in0=po[:], in1=rl[:], op=ALU.mult), [po, rl], [ym])
            gate_store(l, ym, ym[:], 32 + hd, 12 + hd, tt, sl)
            yield

    def u_ag(l, c):
        tts = range(c * CH, (c + 1) * CH)
        k.coll("AllGather", ALU.bypass, GYLc[c], GYFc[c], [GYLb[ct][t] for ct in range(NH) for t in tts], [GYFb[c]])
        yield

    def u_glu(l, tt):
        sl = slice(tt * TT, (tt + 1) * TT)
        c, cs = csl(tt)
        gk = [gyf() for _ in range(4)]
        k.dma_group([(gk[kt][:], GYFc[c][kt * 128:(kt + 1) * 128, cs], [GYFb[c]], [gk[kt]]) for kt in range(4)], key="gyfg")
        yield
        for o_ in range(NH):
            pa = psn(); pb = psn()
            for kt in range(4):
                k.op("pe", lambda e, pa=pa, kt=kt, o_=o_: e.matmul(pa[:], lhsT=wglu[:, kt, o_ * 128:(o_ + 1) * 128], rhs=gk[kt][:], start=(kt == 0), stop=(kt == 3)), [wglu, gk[kt]], [pa], inc=(kt == 3))
            for kt in range(4):
                k.op("pe", lambda e, pb=pb, kt=kt, o_=o_: e.matmul(pb[:], lhsT=wglu[:, kt, 256 + o_ * 128:256 + (o_ + 1) * 128], rhs=gk[kt][:], start=(kt == 0), stop=(kt == 3)), [wglu, gk[kt]], [pb], inc=(kt == 3))
            sgb = t32(); ya = t32()
            k.op(AC, lambda e, sgb=sgb, pb=pb: e.activation(out=sgb[:], in_=pb[:], func=AF.Sigmoid), [pb], [sgb])
            k.op(DV, lambda e, ya=ya, pa=pa, sgb=sgb: e.tensor_tensor(out=ya[:], in0=pa[:], in1=sgb[:], op=ALU.mult), [pa, sgb], [ya])
            gate_store(l, ya, ya[:], 4 + o_, o_, tt, sl)
            yield

    def u_s2a(l, tt):
        dap = dap2[l % 2]
        PJ = PJ2[l % 2]; PJb = PJb2[l % 2]; VS = VS2[l % 2]; VSb = VSb2[l % 2]
        sl = slice(tt * TT, (tt + 1) * TT)
        acc_o = [PS[4], PS[5]]
        units = []
        for kb in range(tt + 1):
            for sub in range(4):
                for c in range(2):
                    units.append((kb, sub, c, (sub * 128 if kb == tt else 0), (kb == 0 and sub == 0), (kb == tt and sub == 3)))
        qts = {}
        tiles = {}
        def load(hd, kb):
            ksl = slice(kb * TT, (kb + 1) * TT)
            kt_ = ld16A()
            k.dma(kt_[:], PJ[12 + hd][:, ksl], reads=[PJb[12 + hd][kb]], writes=[kt_])
            vb = vblk()
            k.dma(vb[:], VS[kb * TT:(kb + 1) * TT, hd * 128:(hd + 1) * 128].rearrange("(s p) d -> p s d", p=128), reads=[VSb[kb]], writes=[vb])
            tiles[(hd, kb)] = (kt_, vb)
        def load_q(hd):
            qt = qtr()
            k.dma(qt[:], PJ[8 + hd][:, sl], reads=[PJb[8 + hd][tt]], writes=[qt])
            qts[hd] = qt
        load_q(0)
        load(0, 0)
        for hd in range(NH):
            qt = qts[hd]
            def emitS(j, qt=qt, hd=hd):
                kb, sub, c, q0, first, last = units[j]
                kt_ = tiles[(hd, kb)][0]
                pss = PS[j % 2]
                k.op("pe", lambda e: e.matmul(pss[:, q0:TT], lhsT=kt_[64 * c:64 * c + 64, sub * 128:(sub + 1) * 128],
                                              rhs=qt[64 * c:64 * c + 64, q0:TT], start=True, stop=True), [kt_, qt], [pss])
            emitS(0)
            emitS(1)
            for j, (kb, sub, c, q0, first, last) in enumerate(units):
                if sub == 0 and c == 0:
                    if kb + 1 <= tt:
                        load(hd, kb + 1)
                    elif hd + 1 < NH:
                        load_q(hd + 1)
                        load(hd + 1, 0)
                vb = tiles[(hd, kb)][1]
                pss = PS[j % 2]
                pe_ = t16()
                k.op(AC, lambda e, pe_=pe_, pss=pss, q0=q0: e.activation(out=pe_[:, q0:TT], in_=pss[:, q0:TT], func=AF.Exp, scale=0.125), [pss], [pe_])
                if kb == tt:
                    k.op(DV, lambda e, pe_=pe_, q0=q0: e.tensor_tensor(out=pe_[:, q0:q0 + 128], in0=pe_[:, q0:q0 + 128], in1=tri[:], op=ALU.mult), [pe_, tri], [pe_])
                if j + 2 < len(units):
                    emitS(j + 2)
                k.op("pe", lambda e, ao=acc_o[c], pe_=pe_, vb=vb, sub=sub, q0=q0, first=first, last=last: e.matmul(ao[:, q0:TT], lhsT=vb[:, sub, :], rhs=pe_[:, q0:TT], start=first, stop=last),
                     [vb, pe_], [acc_o[c]])
                k.op("pe", lambda e, c=c, pe_=pe_, q0=q0, first=first, last=last: e.matmul(LS[32 * c:32 * c + 32, q0:TT], lhsT=ones[:, 0:32], rhs=pe_[:, q0:TT], start=first, stop=last, tile_position=(0, 32 * c)),
                     [ones, pe_], [LSb[c]])
                if c == 1:
                    yield
            l32 = fin_l
            k.op(AC, lambda e, l32=l32: e.activation(out=l32[0:64, :], in_=LS[0:64, :], func=AF.Copy), [LSb[0], LSb[1]], [l32])
            k.op(DV, lambda e, l32=l32: e.reciprocal(out=l32[0:64, :], in_=l32[0:64, :]), [l32], [l32])
            yield
            pl0 = psa(); pl1 = psa()
            k.op("pe", lambda e, pl0=pl0, l32=l32: e.matmul(pl0[:], lhsT=cn[:, 262:390], rhs=l32[:], start=True, stop=True), [cn, l32], [pl0])
            k.op("pe", lambda e, pl1=pl1, l32=l32: e.matmul(pl1[:], lhsT=cn[:, 390:518], rhs=l32[:], start=True, stop=True), [cn, l32], [pl1])
            r0 = t32(); r1 = t32(); o0 = fin_o; o1 = t32()
            k.op(AC, lambda e, r0=r0, pl0=pl0: e.activation(out=r0[:], in_=pl0[:], func=AF.Copy), [pl0], [r0])
            k.op(AC, lambda e, r1=r1, pl1=pl1: e.activation(out=r1[:], in_=pl1[:], func=AF.Copy), [pl1], [r1])
            k.op(DV, lambda e, o0=o0, r0=r0: e.tensor_tensor(out=o0[:], in0=acc_o[0][:], in1=r0[:], op=ALU.mult), [acc_o[0], r0], [o0])
            k.op(DV, lambda e, o1=o1, r1=r1: e.tensor_tensor(out=o1[:], in0=acc_o[1][:], in1=r1[:], op=ALU.mult), [acc_o[1], r1], [o1])
            k.op(DV, lambda e, o0=o0, o1=o1: e.scalar_tensor_tensor(out=o0[:], in0=o1[:], scalar=dap[:, 4:5], in1=o0[:], op0=ALU.mult, op1=ALU.add), [o0, o1, dap], [o0])
            sq_ = fin_sq
            k.op(AC, lambda e, sq_=sq_, o0=o0: e.activation(out=sq_[:], in_=o0[:], func=AF.Square), [o0], [sq_])
            yield
            pn2 = psa()
            k.op("pe", lambda e, pn2=pn2, sq_=sq_: e.matmul(pn2[:], lhsT=ones[:], rhs=sq_[:], start=True, stop=True), [ones, sq_], [pn2])
            rr2 = rms_rstd(pn2, TT, 128)
            yb_ = t32()
            k.op(DV, lambda e, yb_=yb_, o0=o0, rr2=rr2: e.scalar_tensor_tensor(out=yb_[:], in0=o0[:], scalar=dap[:, 5:6], in1=rr2[:], op0=ALU.mult, op1=ALU.mult), [o0, dap, rr2], [yb_])
            gate_store(l, yb_, yb_[:], 16 + hd, 4 + hd, tt, sl)
            yield

    MIXK = [0, 1, 4, 5, 8, 9, 12, 13]

    def u_s3(l, tt):
        MX = MX2[l % 2]; MXb = MXb2[l % 2]
        sl = slice(tt * TT, (tt + 1) * TT)
        c, cs = csl(tt)
        mm = [mxs[8 * (tt % 2) + j] for j in range(8)]
        k.dma_group([(mm[j][:], MX[kt][:, sl], [MXb[kt][tt]], [mm[j]]) for j, kt in enumerate(MIXK)], key=f"mxg{tt % 2}")
        yield
        for ob in range(8):
            wo = wo_s()
            k.dma(wo[:], WB[l, 24 + ob], reads=[WBb[l][24 + ob]], writes=[wo])
            pp = psn()
            for j in range(8):
                k.op("pe", lambda e, pp=pp, wo=wo, j=j: e.matmul(pp[:], lhsT=wo[:, j * 128:(j + 1) * 128], rhs=mm[j][:], start=(j == 0), stop=(j == 7)), [wo, mm[j]], [pp], inc=(j == 7))
            xo = st32()
            if l == 0:
                k.dma(xo[:], xT[ob * 128:(ob + 1) * 128, sl], writes=[xo])
            else:
                k.dma(xo[:], XRc[c][ob * 128:(ob + 1) * 128, cs], reads=[XRb[ob][tt]], writes=[xo])
            k.op(DV, lambda e, xo=xo, pp=pp: e.scalar_tensor_tensor(out=xo[:], in0=xo[:], scalar=0.5, in1=pp[:], op0=ALU.mult, op1=ALU.add), [xo, pp], [xo])
            k.dma(PTc[c][ob * 128:(ob + 1) * 128, cs], xo[:], reads=[xo], writes=[PTb[ob][tt]], q="pool")
            yield

    def u_ar(l, c):
        tts = range(c * CH, (c + 1) * CH)
        for hf in range(2):
            obs = range(4 * hf, 4 * hf + 4)
            k.coll("AllReduce", ALU.add, PTc[c][512 * hf:512 * (hf + 1), :], XRc[c][512 * hf:512 * (hf + 1), :],
                   [PTb[ob][t] for ob in obs for t in tts], [XRb[ob][t] for ob in obs for t in tts])
        yield

    def u_final(tt):
        sl = slice(tt * TT, (tt + 1) * TT)
        x_ = xs()
        k.dma(x_[:], XRv(tt), reads=[XRb[kt][tt] for kt in range(8)], writes=[x_])
        for kt in range(8):
            k.op(AC, lambda e, kt=kt: e.activation(out=sqb[:, kt, :], in_=x_[:, kt, :], func=AF.Square), [x_], [sqb])
        pn = psn()
        for kt in range(8):
            k.op("pe", lambda e, kt=kt: e.matmul(pn[:], lhsT=ones[:], rhs=sqb[:, kt, :], start=(kt == 0), stop=(kt == 7)), [ones, sqb], [pn], inc=(kt == 7))
        rr = rms_rstd(pn, TT, D)
        for kt in range(8):
            k.op(DV, lambda e, kt=kt: e.scalar_tensor_tensor(out=x_[:, kt, :], in0=x_[:, kt, :], scalar=fnt[:, kt:kt + 1], in1=rr[:],
                                                            op0=ALU.mult, op1=ALU.mult), [x_, fnt, rr], [x_])
        k.dma(yTv[:, :, sl], x_[:], reads=[x_], writes=[YB], q="pool")
        yield

    LT = NT - 1
    sS1 = [("init",)]
    for l in range(depth):
        sS1.append(("conv", l))
    for l in range(depth):
        sS1.append(("setupA", l))
        if l == 0:
            sS1.append(("s1m", 0))
        sS1 += [("s1", l, tt) for tt in range(NT)]
        if l > 0:
            sS1.append(("s1m", l))
    sS1 += [("final", tt) for tt in range(NT)]
    sA = [("s2a", l, tt) for l in range(depth) for tt in range(NT)]
    sB = []
    sG = []
    sC = []
    for l in range(depth):
        sB.append(("setupB", l))
        for tt in range(NT):
            sB.append(("s2b", l, tt))
            sG.append(("glu", l, tt))
            sC.append(("s3", l, tt))
            if tt % CH == CH - 1:
                sB.append(("ag", l, tt // CH))
                sC.append(("ar", l, tt // CH))

    def deps(u):
        t = u[0]
        if t == "s1":
            return [("ar", u[1] - 1, u[2] // CH)] if u[1] > 0 else []
        if t == "s1m":
            return [("s2b", u[1] - 1, LT)] if u[1] > 0 else []
        if t == "setupA":
            return [("s2a", u[1] - 2, LT), ("s2b", u[1] - 2, LT)] if u[1] >= 2 else []
        if t == "final":
            return [("ar", depth - 1, u[1] // CH)]
        if t == "setupB":
            d = [("setupA", u[1])]
            if u[1] > 0:
                d += [("s2b", u[1] - 1, LT), ("glu", u[1] - 1, LT)]
            if u[1] > 1:
                d += [("s2a", u[1] - 2, LT)]
            return d
        if t == "s2b":
            return [("s1", u[1], u[2]), ("s1m", u[1]), ("setupB", u[1])]
        if t == "s2a":
            return [("s1", u[1], u[2]), ("setupB", u[1])]
        if t == "glu":
            return [("ag", u[1], u[2] // CH), ("setupB", u[1])]
        if t == "s3":
            return [("s2a", u[1], u[2]), ("s2b", u[1], u[2]), ("glu", u[1], u[2])]
        return []

    def make(u):
        t = u[0]
        return {"init": u_init, "conv": u_conv, "setupA": u_setupA, "s1m": u_s1m, "s1": u_s1, "final": u_final,
                "setupB": u_setupB, "s2b": u_s2b, "s2a": u_s2a, "s3": u_s3, "glu": u_glu, "ag": u_ag, "ar": u_ar}[t](*u[1:])

    done = set()
    streams = [[sS1, 0, None], [sA, 0, None], [sB, 0, None], [sG, 0, None], [sC, 0, None]]
    weights = [1, 2, 1, 1, 1]
    while True:
        progressed = False
        alldone = True
        for si, stt in enumerate(streams):
            lst, idx, gen = stt
            for _ in range(weights[si]):
                lst, idx, gen = stt
                if gen is None:
                    if idx >= len(lst):
                        break
                    u = lst[idx]
                    if all(d in done for d in deps(u)):
                        stt[2] = make(u)
                        gen = stt[2]
                    else:
                        break
                try:
                    next(gen)
                    progressed = True
                except StopIteration:
                    done.add(lst[idx])
                    stt[1] = idx + 1
                    stt[2] = None
                    progressed = True
            if stt[1] < len(stt[0]):
                alldone = False
        if alldone:
            break
        assert progressed, "scheduler stuck"
    k.finish([YB])
    return nc, k


_CACHE = {}


def run_model(inp, L, depth, ncores=8):
    packs = [pack_weights(inp, depth, h) for h in range(2)]
    if (L, depth) not in _CACHE:
        _CACHE[(L, depth)] = build(L, depth)[0]
    nc = _CACHE[(L, depth)]
    B = inp["x"].shape[0]
    in_maps = []
    for c in range(ncores):
        b = (c // 2) % B
        WALL, SP, CN, FN = packs[c % 2]
        in_maps.append({
            "xT": np.ascontiguousarray(np.asarray(inp["x"][b], np.float32).T),
            "memT": np.ascontiguousarray(np.asarray(inp["mem"][b], np.float32).T),
            "pos": np.ascontiguousarray(np.asarray(inp["positions"][b], np.int32).reshape(1, L)),
            "wall": WALL, "sp": SP, "cn": CN, "fn": FN,
        })
    res = run_bass_kernel_spmd(nc, in_maps, core_ids=list(range(ncores)))
    out = np.stack([np.ascontiguousarray(res.results[2 * b]["yT"].T) for b in range(B)], axis=0)
    return out.astype(np.float32)


def kernel(**inputs):
    return run_model(inputs, 8192, 4, 8)
```

```python
import numpy as np
import concourse.bass as bass
import concourse.mybir as mybir
from concourse.bass_utils import run_bass_kernel_spmd
from contextlib import ExitStack

F32 = mybir.dt.float32
BF16 = mybir.dt.bfloat16
I32 = mybir.dt.int32
AF = mybir.ActivationFunctionType
ALU = mybir.AluOpType
AX = mybir.AxisListType

ENGS = ("pe", "act", "dve", "pool", "sp")


class Buf:
    __slots__ = ("name", "w", "r", "ap", "full")

    def __init__(self, name, ap=None):
        self.name = name
        self.w = []
        self.r = []
        self.ap = ap

    def __getitem__(self, k):
        return self.ap[k]


class K:
    def __init__(self, nc):
        self.nc = nc
        self.es = ExitStack()
        self.prog = {e: [] for e in ENGS}
        self.cnt = {}
        self.sems = {}
        self.known = {e: {} for e in ENGS}
        for e in ENGS[:4]:
            self._newsem(e)
        self.dma_sems = []
        self.dma_rr = 0
        self.ninstr = 0

    def _newsem(self, key):
        self.sems[key] = self.es.enter_context(self.nc.semaphore("s_" + str(key)))
        self.cnt[key] = 0

    def sb(self, name, shape, dt):
        t = self.es.enter_context(self.nc.sbuf_tensor("sb_" + name, list(shape), dt))
        return Buf(name, t)

    def ps(self, name, shape, dt=F32):
        t = self.es.enter_context(self.nc.psum_tensor("ps_" + name, list(shape), dt))
        return Buf(name, t)

    def dram(self, name, shape, dt, kind="Internal"):
        t = self.nc.dram_tensor(name, list(shape), dt, kind=kind)
        return t.ap()

    def _waits(self, eng, reads, writes):
        need = {}
        def add(t):
            k, v = t
            if eng == "pe" and k == "pe":
                return
            if need.get(k, 0) < v:
                need[k] = v
        for b in reads:
            for t in b.w:
                add(t)
        for b in writes:
            for t in b.w:
                add(t)
            for t in b.r:
                add(t)
        kn = self.known[eng]
        out = []
        for k, v in need.items():
            if kn.get(k, 0) < v:
                kn[k] = v
                out.append((k, v))
        return out

    def _commit(self, ticket, reads, writes):
        for b in writes:
            b.w = [ticket]
            b.r = []
        for b in reads:
            if b not in writes:
                b.r.append(ticket)
                if len(b.r) > 24:
                    m = {}
                    for k, v in b.r:
                        if m.get(k, 0) < v:
                            m[k] = v
                    b.r = list(m.items())

    def op(self, eng, fn, reads=(), writes=(), inc=True):
        waits = self._waits(eng, reads, writes)
        if inc:
            self.cnt[eng] += 1
            ticket = (eng, self.cnt[eng])
            self.prog[eng].append((waits, fn, (eng, 1)))
        else:
            ticket = (eng, self.cnt[eng] + 1)
            self.prog[eng].append((waits, fn, None))
        self._commit(ticket, reads, writes)
        self.ninstr += 1

    def dma(self, out_ap, in_ap, reads=(), writes=(), q="sp", key=None):
        if key is None:
            key = "dma_" + (writes[0].name if writes and writes[0].ap is not None else reads[0].name)
        if key not in self.sems:
            self._newsem(key)
        waits = self._waits(q, reads, writes)
        prev = self.cnt[key]
        if prev > 0 and self.known[q].get(key, 0) < prev:
            self.known[q][key] = prev
            waits.append((key, prev))
        self.cnt[key] += 16
        ticket = (key, self.cnt[key])
        fn = lambda e, o=out_ap, i=in_ap: e.dma_start(out=o, in_=i)
        self.prog[q].append((waits, fn, (key, 16)))
        self._commit(ticket, reads, writes)
        self.ninstr += 1

    def dma_group(self, items, key, q="sp"):
        if key not in self.sems:
            self._newsem(key)
        prev = self.cnt[key]
        final = prev + 16 * len(items)
        first = True
        for (o, i, reads, writes) in items:
            waits = self._waits(q, reads, writes)
            if first and prev > 0 and self.known[q].get(key, 0) < prev:
                self.known[q][key] = prev
                waits.append((key, prev))
            first = False
            fn = lambda e, o=o, i=i: e.dma_start(out=o, in_=i)
            self.prog[q].append((waits, fn, (key, 16)))
            self._commit((key, final), reads, writes)
            self.ninstr += 1
        self.cnt[key] = final

    def coll(self, kind, alu, in_ap, out_ap, reads, writes):
        key = "cc%d" % len([s for s in self.sems if str(s).startswith("cc")])
        self._newsem(key)
        waits = self._waits("pool", reads, writes)
        self.cnt[key] = 1
        fn = lambda e, i=in_ap, o=out_ap: e.collective_compute(kind, alu, replica_groups=GROUPS, ins=[i], outs=[o])
        self.prog["pool"].append((waits, fn, (key, None)))
        self._commit((key, 1), reads, writes)
        self.ninstr += 1

    def finish(self, final_bufs):
        waits = self._waits("sp", final_bufs, final_bufs)
        self.prog["sp"].append((waits, None, None))
        nc = self.nc
        sems = self.sems
        prog = self.prog

        def run(e, lst):
            for waits, fn, inc in lst:
                for k, v in waits:
                    e.wait_ge(sems[k], v)
                if fn is not None:
                    ins = fn(e)
                    if inc is not None:
                        if inc[1] is None:
                            ins.then_inc(sems[inc[0]])
                        else:
                            ins.then_inc(sems[inc[0]], inc[1])

        with nc.Block() as block:
            @block.tensor
            def _(e):
                run(e, prog["pe"])

            @block.scalar
            def _(e):
                run(e, prog["act"])

            @block.vector
            def _(e):
                run(e, prog["dve"])

            @block.gpsimd
            def _(e):
                run(e, prog["pool"])

            @block.sync
            def _(e):
                run(e, prog["sp"])
        self.es.close()


D = 1024
TT = 512
NCH = 41
NH = 2
CH = 4
GROUPS = [[0, 1], [2, 3], [4, 5], [6, 7]]
NSP = 357
NCN = 518
EPS = 1e-6
TWO_PI = float(2 * np.pi)
import math


def _cols_in(h):
    G = 512
    def seg(i):
        return list(range(i * G + h * 256, i * G + (h + 1) * 256))
    def partner(cols):
        out = []
        for f, c in enumerate(cols):
            d = f % 64
            if d < 8:
                out.append(cols[f + 8])
            elif d < 16:
                out.append(cols[f - 8])
            else:
                out.append(c)
        return out
    a_u, a_g, q, kk, v, b_g, c_x, c_g, m_q, m_g = [seg(i) for i in range(10)]
    order = a_u + a_g + q + partner(q) + kk + partner(kk) + b_g + c_x + c_g + m_q + m_g
    return np.array(order, dtype=np.int64), np.array(v, dtype=np.int64)


def pack_weights(inp, depth, h):
    f32 = np.float32
    cols, vcols = _cols_in(h)
    WALL = np.zeros((depth, NCH, 128, 1024), f32)
    SP = np.zeros((depth, 128, NSP), f32)
    hs = slice(2 * h, 2 * h + 2)
    for l in range(depth):
        w_in = np.asarray(inp["w_in"][l], f32)
        wi = w_in[:, cols].reshape(8, 128, 22, 128)
        WALL[l, 0:22] = wi.transpose(2, 1, 0, 3).reshape(22, 128, 1024)
        wv = w_in[:, vcols].reshape(8, 128, 256)
        WALL[l, 22:24] = wv.reshape(2, 4, 128, 256).transpose(0, 2, 1, 3).reshape(2, 128, 1024)
        rows = np.concatenate([g * 512 + h * 256 + np.arange(256) for g in range(4)])
        wo = np.asarray(inp["w_out"][l], f32)[rows, :].reshape(8, 128, 8, 128)
        WALL[l, 24:32] = wo.transpose(2, 1, 0, 3).reshape(8, 128, 1024)
        gcols = np.concatenate([h * 256 + np.arange(256), 512 + h * 256 + np.arange(256)])
        wg = np.asarray(inp["s5_w_glu"][l], f32)[:, gcols]
        WALL[l, 32:34] = wg.reshape(2, 2, 128, 512).transpose(0, 2, 1, 3).reshape(2, 128, 1024)
        wm = np.asarray(inp["w_mem_kv"][l], f32)
        wk = wm[:, h * 256:(h + 1) * 256].reshape(8, 128, 2, 128)
        WALL[l, 34:36] = wk.transpose(2, 1, 0, 3).reshape(2, 128, 1024)
        wmv = wm[:, 512 + h * 256:512 + (h + 1) * 256].reshape(8, 128, 256)
        WALL[l, 36:38] = wmv.reshape(2, 4, 128, 256).transpose(0, 2, 1, 3).reshape(2, 128, 1024)
        bd = np.zeros((128, 2, 4, 128), f32)
        for wi_, nm in enumerate(("lru_w_a", "lru_w_x")):
            w = np.asarray(inp[nm][l], f32)
            for ct in range(2):
                for b2 in range(2):
                    bd[b2 * 64:(b2 + 1) * 64, wi_, ct, b2 * 64:(b2 + 1) * 64] = w[2 * (2 * h + ct) + b2]
        WALL[l, 38] = bd.reshape(128, 1024)
        bt = np.zeros((4, 2, 16, 2, 4, 2, 64), f32)
        ctt = np.zeros((2, 64, 2, 16, 2, 16), f32)
        for ri, (bn, cn) in enumerate((("s5_b_re", "s5_c_re"), ("s5_b_im", "s5_c_im"))):
            b = np.asarray(inp[bn][l], f32)
            c = np.asarray(inp[cn][l], f32)
            for st in range(8):
                ct, q = st // 4, st % 4
                for gl in range(2):
                    g = 2 * (8 * h + st) + gl
                    bt[q, gl, :, ri, ct, gl, :] = b[g].T
                    ctt[gl, :, ri, st, gl, :] = c[g].T
        WALL[l, 39] = bt.reshape(128, 1024)
        WALL[l, 40] = ctt.reshape(128, 1024)
        SP[l, :, 0:8] = np.asarray(inp["norm_g"][l], f32).reshape(8, 128).T
        SP[l, :, 8:16] = np.asarray(inp["mem_norm_g"][l], f32).reshape(8, 128).T
        lr = np.asarray(inp["s5_lambda_re"][l], f32).reshape(16, 2 * 64).T
        li = np.asarray(inp["s5_lambda_im"][l], f32).reshape(16, 2 * 64).T
        ld = np.repeat(np.asarray(inp["s5_log_dt"][l], f32).reshape(16, 2, 1), 64, axis=2).reshape(16, 128).T
        for rep in range(2):
            SP[l, :, 16 + 8 * rep:24 + 8 * rep] = lr[:, 8 * h:8 * h + 8]
            SP[l, :, 32 + 8 * rep:40 + 8 * rep] = li[:, 8 * h:8 * h + 8]
            SP[l, :, 48 + 8 * rep:56 + 8 * rep] = ld[:, 8 * h:8 * h + 8]
        SP[l, :, 64:66] = np.asarray(inp["s5_d"][l], f32).reshape(4, 128).T[:, hs]
        SP[l, :, 68] = np.asarray(inp["da_subln_g"][l], f32)
        cw = np.asarray(inp["lru_conv_w"][l], f32).reshape(4, 4, 128)
        SP[l, :, 69:77] = cw.transpose(2, 1, 0)[:, hs, :].reshape(128, 8)
        SP[l, :, 85:87] = np.asarray(inp["lru_conv_b"][l], f32).reshape(4, 128).T[:, hs]
        SP[l, :, 89:91] = np.asarray(inp["lru_b_a"][l], f32).reshape(4, 128).T[:, hs]
        SP[l, :, 93:95] = np.asarray(inp["lru_b_x"][l], f32).reshape(4, 128).T[:, hs]
        SP[l, :, 97:99] = np.asarray(inp["lru_lambda"][l], f32).reshape(4, 128).T[:, hs]
        for i, nm in enumerate(("da_lambda_q1", "da_lambda_k1", "da_lambda_q2", "da_lambda_k2")):
            SP[l, :, 101 + 64 * i:101 + 64 * (i + 1)] = np.asarray(inp[nm][l], f32)[None, :]
    CN = np.zeros((128, NCN), f32)
    CN[:, 0:129] = np.arange(129, dtype=f32)[None, :]
    for p in range(128):
        d = p % 64
        if d < 16:
            i = d % 8
            CN[p, 129] = 500000.0 ** (-(2.0 * i) / 16.0)
            CN[p, 130] = -1.0 if d < 8 else 1.0
        else:
            CN[p, 129] = 0.0
            CN[p, 130] = 1.0
    CN[:, 131] = EPS
    CN[:, 132] = 0.0
    CN[:, 133] = 1.0
    CN[:, 134:262] = (np.arange(128)[:, None] <= np.arange(128)[None, :]).astype(f32)
    CN[0, 262:390] = 1.0
    CN[32, 390:518] = 1.0
    FN = np.asarray(inp["final_norm_g"], f32).reshape(8, 128).T.copy()
    return WALL, SP, CN, FN


def build(L, depth, debug=False):
    NT = L // TT
    nc = bass.Bass("TRN2", target_bir_lowering=False)
    k = K(nc)
    xT = nc.dram_tensor("xT", [D, L], F32, kind="ExternalInput").ap()
    memT = nc.dram_tensor("memT", [D, 256], F32, kind="ExternalInput").ap()
    posd = nc.dram_tensor("pos", [1, L], I32, kind="ExternalInput").ap()
    walld = nc.dram_tensor("wall", [depth, NCH, 128, 1024], F32, kind="ExternalInput").ap()
    spd = nc.dram_tensor("sp", [depth, 128, NSP], F32, kind="ExternalInput").ap()
    cnd = nc.dram_tensor("cn", [128, NCN], F32, kind="ExternalInput").ap()
    fnd = nc.dram_tensor("fn", [128, 8], F32, kind="ExternalInput").ap()
    yT = nc.dram_tensor("yT", [D, L], F32, kind="ExternalOutput").ap()
    WB = k.dram("WB", [depth, NCH, 128, 1024], BF16)
    NC_ = NT // CH
    CW = CH * TT
    XRc = [k.dram(f"XR{c}", [D, CW], F32) for c in range(NC_)]
    PTc = [k.dram(f"PT{c}", [D, CW], F32) for c in range(NC_)]
    GYLc = [k.dram(f"GYL{c}", [256, CW], BF16) for c in range(NC_)]
    GYFc = [k.dram(f"GYF{c}", [512, CW], BF16) for c in range(NC_)]
    PJ2 = [k.dram(f"PJ{i}", [36, 128, L], BF16) for i in range(2)]
    VS2 = [k.dram(f"VS{i}", [L, 256], BF16) for i in range(2)]
    MX2 = [k.dram(f"MX{i}", [16, 128, L], BF16) for i in range(2)]
    WBb = [[Buf(f"WB{l}_{c}") for c in range(NCH)] for l in range(depth)]
    XRb = [[Buf(f"XR{kt}_{t}") for t in range(NT)] for kt in range(8)]
    PTb = [[Buf(f"PT{kt}_{t}") for t in range(NT)] for kt in range(8)]
    GYLb = [[Buf(f"GYL{ct}_{t}") for t in range(NT)] for ct in range(NH)]
    GYFb = [Buf(f"GYF{c}") for c in range(NC_)]
    PJb2 = [[[Buf(f"PJ{i}_{b}_{t}") for t in range(NT)] for b in range(36)] for i in range(2)]
    VSb2 = [[Buf(f"VS{i}_{t}") for t in range(NT)] for i in range(2)]
    MXb2 = [[[Buf(f"MX{i}_{b}_{t}") for t in range(NT)] for b in range(16)] for i in range(2)]
    YB = Buf("YB")
    xTv = xT.rearrange("(kt p) t -> p kt t", p=128)
    yTv = yT.rearrange("(kt p) t -> p kt t", p=128)
    def csl(tt):
        o = (tt % CH) * TT
        return tt // CH, slice(o, o + TT)
    def XRv(tt):
        c, s = csl(tt)
        return XRc[c].rearrange("(kt p) t -> p kt t", p=128)[:, :, s]

    def ring(name, n, shape, dt):
        bufs = [k.sb(f"{name}{i}", shape, dt) for i in range(n)]
        st = {"i": 0}
        def nxt():
            b = bufs[st["i"] % n]
            st["i"] += 1
            return b
        return nxt
    cn = k.sb("cn", [128, NCN], F32)
    spt2 = [k.sb(f"spt{i}", [128, NSP], F32) for i in range(2)]
    fnt = k.sb("fnt", [128, 8], F32)
    ones = k.sb("ones", [128, 128], BF16)
    ones32 = k.sb("ones32", [128, 128], F32)
    tri = k.sb("tri", [128, 128], BF16)
    xs = ring("xs", 1, [128, 8, TT], F32)
    sqb = k.sb("sqb", [128, 8, TT], BF16)
    hb = k.sb("hb", [128, 8, TT], BF16)
    wslot = ring("wsl", 4, [128, 1024], BF16)
    def ring_v(name, n, shape, dt, w):
        bufs = []
        for i in range(n):
            b = k.sb(f"{name}{i}", shape, dt)
            b.full = b.ap
            b.ap = b.ap[:, 0:w]
            bufs.append(b)
        st = {"i": 0}
        def nxt():
            b = bufs[st["i"] % n]
            st["i"] += 1
            return b
        return nxt
    t32 = ring_v("t32_", 8, [128, 516], F32, TT)
    tsb = t32
    t16 = ring("t16_", 4, [128, TT], BF16)
    xr16 = ring("xr16_", 4, [128, TT], BF16)
    ld16 = ring("ld16_", 3, [128, TT], BF16)
    ld16A = ring("ld16A_", 3, [128, TT], BF16)
    ub = ring("ub_", 2, [128, TT], BF16)
    st16 = ring("st16_", 4, [128, TT], BF16)
    st32 = ring("st32_", 2, [128, TT], F32)
    posi = k.sb("posi", [128, TT], I32)
    kint = k.sb("kint", [128, 516], I32)
    cosT = k.sb("cosT", [128, TT], F32)
    sinT = k.sb("sinT", [128, TT], F32)
    COS = k.sb("COS", [128, 16, 129], F32)
    SIN = k.sb("SIN", [128, 16, 129], F32)
    GRE = k.sb("GRE", [128, 16, 128], F32)
    GIM = k.sb("GIM", [128, 16, 128], F32)
    RPAT = k.sb("RPAT", [128, 16, 128], F32)
    s5p = k.sb("s5p", [128, 16, 16], F32)
    s5i = k.sb("s5i", [128, 2, 16], F32)
    s5tmp = ring("s5tmp", 4, [128, 1], F32)
    btb = k.sb("btb", [128, 1024], BF16)
    ctb = k.sb("ctb", [128, 1024], BF16)
    wglu = k.sb("wglu", [128, 4, 512], BF16)
    gyf = ring("gyf", 4, [128, TT], BF16)
    gy = [k.sb(f"gy{i}", [128, TT], BF16) for i in range(4)]
    wre = ring("wre", 2, [128, TT], F32)
    wim = ring("wim", 2, [128, TT], F32)
    lrub = k.sb("lrub", [128, 1024], BF16)
    cxb = [k.sb(f"cxb{i}", [128, 3 + TT], BF16) for i in range(4)]
    lcar = k.sb("lcar", [128, 4], F32)
    lsc = k.sb("lsc", [128, 4], F32)
    kmem = k.sb("kmem", [128, 4, 256], BF16)
    vmem = k.sb("vmem", [128, 2, 512], BF16)
    dap2 = [k.sb(f"dap{i}", [128, 8], F32) for i in range(2)]
    vblk = ring("vblk", 3, [128, 4, 128], BF16)
    qtr = ring("qtr", 2, [128, TT], BF16)
    fin_l = k.sb("fin_l", [128, TT], F32)
    fin_o = k.sb("fin_o", [128, TT], F32)
    fin_sq = k.sb("fin_sq", [128, TT], BF16)
    wo_s = ring("wo_s", 2, [128, 1024], BF16)
    mxs = [k.sb(f"mxs{i}", [128, TT], BF16) for i in range(16)]
    PS = [k.ps(f"ps{i}", [128, TT], F32) for i in range(8)]
    ROT = [PS[2], PS[3]]
    LS = PS[7]
    LSb = [Buf("LS0"), Buf("LS1")]

    psr = {"i": 0, "a": 0}
    def psn():
        b = ROT[psr["i"] % len(ROT)]
        psr["i"] += 1
        return b
    def psa():
        b = PS[psr["a"] % 2]
        psr["a"] += 1
        return b

    DV = "dve"
    AC = "act"
    PL = "pool"

    def cst(col):
        return cn[:, col:col + 1]

    def range_reduce_sin(out_ap, ang_ap, ki_ap, kf_ap, bufs_r, bufs_w, scale=1.0):
        k.op(DV, lambda e: e.tensor_scalar(out=ki_ap, in0=ang_ap, scalar1=float(1.0 / TWO_PI), scalar2=None, op0=ALU.mult), bufs_r, bufs_w["ki"])
        k.op(DV, lambda e: e.tensor_copy(out=kf_ap, in_=ki_ap), bufs_w["ki"], bufs_w["kf"])
        k.op(DV, lambda e: e.scalar_tensor_tensor(out=kf_ap, in0=kf_ap, scalar=-TWO_PI, in1=ang_ap, op0=ALU.mult, op1=ALU.add), bufs_w["kf"] + bufs_r, bufs_w["kf"])
        k.op(AC, lambda e: e.activation(out=out_ap, in_=kf_ap, func=AF.Sin, scale=scale), bufs_w["kf"], bufs_w["out"])

    def rms_rstd(ps_buf, ncols, dim):
        r = t32()
        k.op(AC, lambda e: e.activation(out=r[:, 0:ncols], in_=ps_buf[:, 0:ncols], func=AF.Sqrt, bias=cst(131), scale=float(1.0 / dim)), [ps_buf, cn], [r])
        k.op(DV, lambda e: e.reciprocal(out=r[:, 0:ncols], in_=r[:, 0:ncols]), [r], [r])
        return r

    def u_init():
        k.dma(cn[:], cnd, writes=[cn])
        k.dma(fnt[:], fnd, writes=[fnt])
        k.op(DV, lambda e: e.memset(ones[:], 1.0), [], [ones])
        k.op(DV, lambda e: e.memset(ones32[:], 1.0), [], [ones32])
        k.op(DV, lambda e: e.memset(fin_l[:], 0.0), [], [fin_l])
        k.op(DV, lambda e: e.tensor_copy(out=tri[:], in_=cn[:, 134:262]), [cn], [tri])
        yield

    def u_conv(l):
        ci = 0
        for c0 in range(0, NCH, 4):
            n = min(4, NCH - c0)
            xb_ = xs()
            xv = xb_[:].rearrange("p a t -> p (a t)").rearrange("p (c n) -> p c n", c=4)
            k.dma(xv[:, 0:n, :], walld[l, c0:c0 + n].rearrange("c p n -> p c n"), writes=[xb_])
            for j in range(n):
                wb = wslot()
                if ci % 2 == 0:
                    k.op(DV, lambda e, wb=wb, xv=xv, j=j: e.tensor_copy(out=wb[:], in_=xv[:, j, :]), [xb_], [wb])
                else:
                    k.op(AC, lambda e, wb=wb, xv=xv, j=j: e.activation(out=wb[:], in_=xv[:, j, :], func=AF.Copy), [xb_], [wb])
                k.dma(WB[l, c0 + j], wb[:], reads=[wb], writes=[WBb[l][c0 + j]], q="pool")
                ci += 1
            yield

    def u_setupA(l):
        S = spt2[l % 2]
        k.dma(S[:], spd[l], writes=[S])
        yield

    def u_s1m(l):
        S = spt2[l % 2]
        memxb = xs()
        memx = memxb[:, :, 0:256]
        memn = hb[:, :, 0:256]
        k.dma(memx, memT.rearrange("(kt p) m -> p kt m", p=128), writes=[memxb])
        pm = psn()
        for kt in range(8):
            k.op(AC, lambda e, kt=kt: e.activation(out=sqb[:, kt, 0:256], in_=memx[:, kt, :], func=AF.Square), [memxb], [sqb])
        for kt in range(8):
            k.op("pe", lambda e, kt=kt, pm=pm: e.matmul(pm[:, 0:256], lhsT=ones[:], rhs=sqb[:, kt, 0:256], start=(kt == 0), stop=(kt == 7)),
                 [ones, sqb], [pm], inc=(kt == 7))
        rm = rms_rstd(pm, 256, D)
        for kt in range(8):
            k.op(DV, lambda e, kt=kt, rm=rm: e.scalar_tensor_tensor(out=memn[:, kt, :], in0=memx[:, kt, :], scalar=S[:, 8 + kt:9 + kt], in1=rm[:, 0:256],
                                                                      op0=ALU.mult, op1=ALU.mult), [memxb, S, rm], [hb])
        for hd in range(NH):
            w = wslot()
            k.dma(w[:], WB[l, 34 + hd], reads=[WBb[l][34 + hd]], writes=[w])
            pk = psn()
            for kt in range(8):
                k.op("pe", lambda e, kt=kt, w=w, pk=pk: e.matmul(pk[:, 0:256], lhsT=w[:, kt * 128:(kt + 1) * 128], rhs=memn[:, kt, :], start=(kt == 0), stop=(kt == 7)),
                     [w, hb], [pk], inc=(kt == 7))
            k.op(AC, lambda e, hd=hd, pk=pk: e.activation(out=kmem[:, hd, :], in_=pk[:, 0:256], func=AF.Copy), [pk], [kmem])
        wv4 = [wslot() for _ in range(2)]
        for j in range(2):
            k.dma(wv4[j][:], WB[l, 36 + j], reads=[WBb[l][36 + j]], writes=[wv4[j]])
        for mt in range(2):
            pv = psn()
            for kt in range(8):
                w = wv4[kt // 4]
                k.op("pe", lambda e, kt=kt, w=w, pv=pv, mt=mt: e.matmul(pv[:, 0:256], lhsT=memn[:, kt, mt * 128:(mt + 1) * 128], rhs=w[:, (kt % 4) * 256:(kt % 4 + 1) * 256],
                                                                        start=(kt == 0), stop=(kt == 7)), [w, hb], [pv], inc=(kt == 7))
            k.op(AC, lambda e, mt=mt, pv=pv: e.activation(out=vmem[:, mt, 0:256], in_=pv[:, 0:256], func=AF.Copy), [pv], [vmem])
        yield

    def u_setupB(l):
        S = spt2[l % 2]
        dap = dap2[l % 2]
        lam_init = 0.8 - 0.6 * math.exp(-0.3 * l)
        tq = t32()
        k.op(DV, lambda e: e.tensor_tensor(out=tq[:, 0:64], in0=S[:, 101:165], in1=S[:, 165:229], op=ALU.mult), [S], [tq])
        k.op(DV, lambda e: e.reduce_sum(out=dap[:, 0:1], in_=tq[:, 0:64], axis=AX.X), [tq], [dap])
        k.op(DV, lambda e: e.tensor_tensor(out=tq[:, 64:128], in0=S[:, 229:293], in1=S[:, 293:357], op=ALU.mult), [S], [tq])
        k.op(DV, lambda e: e.reduce_sum(out=dap[:, 1:2], in_=tq[:, 64:128], axis=AX.X), [tq], [dap])
        k.op(AC, lambda e: e.activation(out=dap[:, 2:4], in_=dap[:, 0:2], func=AF.Exp), [dap], [dap])
        k.op(DV, lambda e: e.tensor_tensor(out=dap[:, 4:5], in0=dap[:, 3:4], in1=dap[:, 2:3], op=ALU.subtract), [dap], [dap])
        k.op(DV, lambda e: e.tensor_scalar(out=dap[:, 4:5], in0=dap[:, 4:5], scalar1=float(-lam_init), scalar2=None, op0=ALU.add), [dap], [dap])
        k.op(DV, lambda e: e.tensor_scalar(out=dap[:, 5:6], in0=S[:, 68:69], scalar1=float(1.0 - lam_init), scalar2=None, op0=ALU.mult), [S], [dap])
        k.op(AC, lambda e: e.activation(out=lsc[:], in_=S[:, 97:101], func=AF.Exp, scale=-1.0), [S], [lsc])
        k.op(AC, lambda e: e.activation(out=lsc[:], in_=lsc[:], func=AF.Ln, bias=cst(133), scale=1.0), [lsc, cn], [lsc])
        k.op(DV, lambda e: e.tensor_scalar(out=lsc[:], in0=lsc[:], scalar1=-8.0, scalar2=None, op0=ALU.mult), [lsc], [lsc])
        k.op(DV, lambda e: e.memset(lcar[:], 0.0), [], [lcar])
        for ct in range(NH):
            k.op(DV, lambda e, ct=ct: e.memset(cxb[ct][:, 0:3], 0.0), [], [cxb[ct]])
        k.dma(lrub[:], WB[l, 38], reads=[WBb[l][38]], writes=[lrub])
        LR = S[:, 16:32]; LI = S[:, 32:48]; LD = S[:, 48:64]
        P = lambda i: s5p[:, i, :]
        k.op(AC, lambda e: e.activation(out=P(0), in_=LD, func=AF.Exp), [S], [s5p])
        k.op(DV, lambda e: e.tensor_tensor(out=P(8), in0=LR, in1=P(0), op=ALU.mult), [S, s5p], [s5p])
        k.op(AC, lambda e: e.activation(out=P(1), in_=P(8), func=AF.Exp), [s5p], [s5p])
        k.op(DV, lambda e: e.tensor_tensor(out=P(2), in0=LI, in1=P(0), op=ALU.mult), [S, s5p], [s5p])
        for g4 in range(2):
            ss = slice(4 * g4, 4 * g4 + 4)
            b1 = tsb(); b2 = tsb()
            b1v3 = b1.full[:, 0:516].rearrange("p (s j) -> p s j", s=4)
            b2v3 = b2.full[:, 0:516].rearrange("p (s j) -> p s j", s=4)
            kiv = kint[:, 0:516].rearrange("p (s j) -> p s j", s=4)
            k.op(DV, lambda e, b1v3=b1v3, ss=ss: e.tensor_tensor(out=b1v3, in0=s5p[:, 2, ss].unsqueeze(2).to_broadcast([128, 4, 129]),
                                               in1=cn[:, 0:129].unsqueeze(1).to_broadcast([128, 4, 129]), op=ALU.mult), [s5p, cn], [b1])
            range_reduce_sin(SIN[:, ss, :], b1v3, kiv, b2v3, [b1], {"ki": [kint], "kf": [b2], "out": [SIN]})
            k.op(DV, lambda e, b1v3=b1v3: e.tensor_scalar(out=b1v3, in0=b1v3, scalar1=float(np.pi / 2), scalar2=None, op0=ALU.add), [b1], [b1])
            range_reduce_sin(COS[:, ss, :], b1v3, kiv, b2v3, [b1], {"ki": [kint], "kf": [b2], "out": [COS]})
        k.op(DV, lambda e: e.tensor_tensor(out=P(3)[:, 0:8], in0=P(1)[:, 0:8], in1=COS[:, 0:8, 1], op=ALU.mult), [s5p, COS], [s5p])
        k.op(DV, lambda e: e.tensor_scalar(out=P(3), in0=P(3), scalar1=-1.0, scalar2=None, op0=ALU.add), [s5p], [s5p])
        k.op(DV, lambda e: e.tensor_tensor(out=P(4)[:, 0:8], in0=P(1)[:, 0:8], in1=SIN[:, 0:8, 1], op=ALU.mult), [s5p, SIN], [s5p])
        k.op(DV, lambda e: e.tensor_tensor(out=P(5), in0=LR, in1=LR, op=ALU.mult), [S], [s5p])
        k.op(DV, lambda e: e.tensor_tensor(out=P(8), in0=LI, in1=LI, op=ALU.mult), [S], [s5p])
        k.op(DV, lambda e: e.tensor_tensor(out=P(5), in0=P(5), in1=P(8), op=ALU.add), [s5p], [s5p])
        k.op(DV, lambda e: e.reciprocal(out=P(5), in_=P(5)), [s5p], [s5p])
        k.op(DV, lambda e: e.tensor_tensor(out=P(8), in0=P(3), in1=LR, op=ALU.mult), [s5p, S], [s5p])
        k.op(DV, lambda e: e.tensor_tensor(out=P(9), in0=P(4), in1=LI, op=ALU.mult), [s5p, S], [s5p])
        k.op(DV, lambda e: e.tensor_tensor(out=P(8), in0=P(8), in1=P(9), op=ALU.add), [s5p], [s5p])
        k.op(DV, lambda e: e.tensor_tensor(out=P(6), in0=P(8), in1=P(5), op=ALU.mult), [s5p], [s5p])
        k.op(DV, lambda e: e.tensor_tensor(out=P(8), in0=P(4), in1=LR, op=ALU.mult), [s5p, S], [s5p])
        k.op(DV, lambda e: e.tensor_tensor(out=P(9), in0=P(3), in1=LI, op=ALU.mult), [s5p, S], [s5p])
        k.op(DV, lambda e: e.tensor_tensor(out=P(8), in0=P(8), in1=P(9), op=ALU.subtract), [s5p], [s5p])
        k.op(DV, lambda e: e.tensor_tensor(out=P(7), in0=P(8), in1=P(5), op=ALU.mult), [s5p], [s5p])
        for g4 in range(2):
            ss = slice(4 * g4, 4 * g4 + 4)
            bc = lambda i, ss=ss: s5p[:, i, ss].unsqueeze(2).to_broadcast([128, 4, 128])
            b1 = tsb(); b2 = tsb()
            b1v = b1[:, 0:512].rearrange("p (s j) -> p s j", s=4)
            b2v = b2[:, 0:512].rearrange("p (s j) -> p s j", s=4)
            C128 = COS[:, ss, 0:128]; S128 = SIN[:, ss, 0:128]
            k.op(DV, lambda e, b1v=b1v, C128=C128, bc=bc: e.tensor_tensor(out=b1v, in0=C128, in1=bc(6), op=ALU.mult), [COS, s5p], [b1])
            k.op(DV, lambda e, b2v=b2v, S128=S128, bc=bc: e.tensor_tensor(out=b2v, in0=S128, in1=bc(7), op=ALU.mult), [SIN, s5p], [b2])
            k.op(DV, lambda e, b1v=b1v, b2v=b2v, ss=ss: e.tensor_tensor(out=GRE[:, ss, :], in0=b1v, in1=b2v, op=ALU.add), [b1, b2], [GRE])
            k.op(DV, lambda e, b1v=b1v, C128=C128, bc=bc: e.tensor_tensor(out=b1v, in0=C128, in1=bc(7), op=ALU.mult), [COS, s5p], [b1])
            k.op(DV, lambda e, b2v=b2v, S128=S128, bc=bc: e.tensor_tensor(out=b2v, in0=S128, in1=bc(6), op=ALU.mult), [SIN, s5p], [b2])
            k.op(DV, lambda e, b1v=b1v, b2v=b2v, ss=ss: e.tensor_tensor(out=GIM[:, ss, :], in0=b1v, in1=b2v, op=ALU.subtract), [b1, b2], [GIM])
            k.op(DV, lambda e, ss=ss, bc=bc: e.tensor_copy(out=RPAT[:, ss, :], in_=bc(1)), [s5p], [RPAT])
        k.op(DV, lambda e: e.memset(s5i[:], 0.0), [], [s5i])
        k.dma(btb[:], WB[l, 39], reads=[WBb[l][39]], writes=[btb])
        k.dma(ctb[:], WB[l, 40], reads=[WBb[l][40]], writes=[ctb])
        k.op(DV, lambda e: e.tensor_scalar(out=ctb[:, 512:1024], in0=ctb[:, 512:1024], scalar1=-1.0, scalar2=None, op0=ALU.mult), [ctb], [ctb])
        for c2 in range(2):
            k.dma(wglu[:, 2 * c2:2 * c2 + 2, :], WB[l, 32 + c2].rearrange("p (a n) -> p a n", a=2), reads=[WBb[l][32 + c2]], writes=[wglu])
        yield

    btv = btb[:].rearrange("p (r c m) -> p r c m", r=2, c=4)
    ctv = ctb[:].rearrange("p (r s m) -> p r s m", r=2, s=16)
    v4 = lambda ap: ap.rearrange("p (c j) -> p c j", c=4)

    def u_s1(l, tt):
        S = spt2[l % 2]
        PJ = PJ2[l % 2]; PJb = PJb2[l % 2]; VS = VS2[l % 2]; VSb = VSb2[l % 2]
        sl = slice(tt * TT, (tt + 1) * TT)
        x_ = xs()
        if l == 0:
            k.dma(x_[:], xTv[:, :, sl], writes=[x_])
        else:
            k.dma(x_[:], XRv(tt), reads=[XRb[kt][tt] for kt in range(8)], writes=[x_])
        for kt in range(8):
            k.op(AC, lambda e, kt=kt: e.activation(out=sqb[:, kt, :], in_=x_[:, kt, :], func=AF.Square), [x_], [sqb])
        pn = psn()
        for kt in range(8):
            k.op("pe", lambda e, kt=kt: e.matmul(pn[:], lhsT=ones[:], rhs=sqb[:, kt, :], start=(kt == 0), stop=(kt == 7)), [ones, sqb], [pn], inc=(kt == 7))
        rr = rms_rstd(pn, TT, D)
        for kt in range(8):
            k.op(DV, lambda e, kt=kt: e.scalar_tensor_tensor(out=hb[:, kt, :], in0=x_[:, kt, :], scalar=S[:, kt:kt + 1], in1=rr[:],
                                                            op0=ALU.mult, op1=ALU.mult), [x_, S, rr], [hb])
        k.dma(posi[:], posd[0:1, sl].to_broadcast([128, TT]), writes=[posi])
        ang = t32(); kf = t32()
        k.op(DV, lambda e: e.tensor_copy(out=ang[:], in_=posi[:]), [posi], [ang])
        k.op(DV, lambda e: e.tensor_scalar(out=ang[:], in0=ang[:], scalar1=cst(129), scalar2=None, op0=ALU.mult), [ang, cn], [ang])
        kiv2 = kint[:, 0:TT]
        range_reduce_sin(sinT[:], ang[:], kiv2, kf[:], [ang], {"ki": [kint], "kf": [kf], "out": [sinT]}, scale=cst(130))
        k.op(DV, lambda e: e.tensor_scalar(out=ang[:], in0=ang[:], scalar1=float(np.pi / 2), scalar2=None, op0=ALU.add), [ang], [ang])
        range_reduce_sin(cosT[:], ang[:], kiv2, kf[:], [ang], {"ki": [kint], "kf": [kf], "out": [cosT]})
        yield

        plain = [(0, 0), (1, 1), (2, 4), (3, 5)] + [(12 + 2 * g + i, 16 + 4 * g + i) for g in range(5) for i in range(2)]
        order = [b for (b, _) in plain]
        for base in (4, 8):
            for hd in range(NH):
                order += [base + hd, base + 2 + hd]
        wq = {}
        nxt_ld = {"i": 0}
        def prefetch(upto):
            while nxt_ld["i"] < min(upto, len(order)):
                b = order[nxt_ld["i"]]
                w = wslot()
                k.dma(w[:], WB[l, b], reads=[WBb[l][b]], writes=[w])
                wq[nxt_ld["i"]] = w
                nxt_ld["i"] += 1
        pos_ = {"i": 0}
        def proj_block(blk):
            i = pos_["i"]
            assert order[i] == blk
            prefetch(i + 3)
            w = wq.pop(i)
            pos_["i"] += 1
            pp = psn()
            for kt in range(8):
                k.op("pe", lambda e, kt=kt, w=w, pp=pp: e.matmul(pp[:], lhsT=w[:, kt * 128:(kt + 1) * 128], rhs=hb[:, kt, :], start=(kt == 0), stop=(kt == 7)),
                     [w, hb], [pp], inc=(kt == 7))
            return pp
        ev = 0
        for (blk, slot) in plain:
            pp = proj_block(blk)
            o = st16()
            if ev % 2 == 0:
                k.op(AC, lambda e, o=o, pp=pp: e.activation(out=o[:], in_=pp[:], func=AF.Copy), [pp], [o])
            else:
                k.op(DV, lambda e, o=o, pp=pp: e.tensor_copy(out=o[:], in_=pp[:]), [pp], [o])
            ev += 1
            k.dma(PJ[slot][:, sl], o[:], reads=[o], writes=[PJb[slot][tt]], q="pool")
            if ev % 2 == 0:
                yield
        for base, slot0 in ((4, 8), (8, 12)):
            for hd in range(NH):
                p1 = proj_block(base + hd)
                p2 = proj_block(base + 2 + hd)
                a1 = t32(); a2 = t32(); o = st16()
                k.op(DV, lambda e, a1=a1, p1=p1: e.tensor_tensor(out=a1[:], in0=p1[:], in1=cosT[:], op=ALU.mult), [p1, cosT], [a1])
                k.op(DV, lambda e, a2=a2, p2=p2: e.tensor_tensor(out=a2[:], in0=p2[:], in1=sinT[:], op=ALU.mult), [p2, sinT], [a2])
                k.op(DV, lambda e, a1=a1, a2=a2, o=o: e.tensor_tensor(out=o[:], in0=a1[:], in1=a2[:], op=ALU.add), [a1, a2], [o])
                k.dma(PJ[slot0 + hd][:, sl], o[:], reads=[o], writes=[PJb[slot0 + hd][tt]], q="pool")
                yield
        wv = [wslot() for _ in range(2)]
        for j in range(2):
            k.dma(wv[j][:], WB[l, 22 + j], reads=[WBb[l][22 + j]], writes=[wv[j]])
        for sub in range(4):
            pv = psn()
            for kt in range(8):
                w = wv[kt // 4]
                k.op("pe", lambda e, kt=kt, w=w, pv=pv, sub=sub: e.matmul(pv[:, 0:256], lhsT=hb[:, kt, sub * 128:(sub + 1) * 128], rhs=w[:, (kt % 4) * 256:(kt % 4 + 1) * 256],
                                                                          start=(kt == 0), stop=(kt == 7)), [w, hb], [pv], inc=(kt == 7))
            o = st16()
            k.op(AC, lambda e, o=o, pv=pv: e.activation(out=o[:, 0:256], in_=pv[:, 0:256], func=AF.Copy), [pv], [o])
            k.dma(VS[tt * TT + sub * 128: tt * TT + (sub + 1) * 128, :], o[:, 0:256], reads=[o], writes=[VSb[tt]], q="pool")
        yield

    def gate_store(l, ysrc_buf, ysrc_ap, gslot, mslot, tt, sl):
        PJ = PJ2[l % 2]; PJb = PJb2[l % 2]; MX = MX2[l % 2]; MXb = MXb2[l % 2]
        g = ld16()
        k.dma(g[:], PJ[gslot][:, sl], reads=[PJb[gslot][tt]], writes=[g])
        sg = t32()
        k.op(AC, lambda e: e.activation(out=sg[:], in_=g[:], func=AF.Silu), [g], [sg])
        o = st16()
        k.op(DV, lambda e: e.tensor_tensor(out=o[:], in0=ysrc_ap, in1=sg[:], op=ALU.mult), [ysrc_buf, sg], [o])
        k.dma(MX[mslot][:, sl], o[:], reads=[o], writes=[MXb[mslot][tt]], q="pool")

    def u_s2b(l, tt):
        S = spt2[l % 2]
        PJ = PJ2[l % 2]; PJb = PJb2[l % 2]
        sl = slice(tt * TT, (tt + 1) * TT)
        us = {}
        def load_u(ct):
            u = ub()
            k.dma(u[:], PJ[ct][:, sl], reads=[PJb[ct][tt]], writes=[u])
            us[ct] = u
        load_u(0)
        cchunk, csl_ = csl(tt)
        for ct in range(NH):
            u = us[ct]
            if ct + 1 < NH:
                load_u(ct + 1)
            yps = PS[6]
            pend = None
            for q in range(4):
                st = 4 * ct + q
                pre = psn(); pim = psn()
                k.op("pe", lambda e, pre=pre, u=u, q=q, ct=ct: e.matmul(pre[:], lhsT=btv[32 * q:32 * q + 32, 0, ct, :], rhs=u[32 * q:32 * q + 32, :], start=True, stop=True, tile_position=(32 * q, 0)), [btb, u], [pre])
                k.op("pe", lambda e, pim=pim, u=u, q=q, ct=ct: e.matmul(pim[:], lhsT=btv[32 * q:32 * q + 32, 1, ct, :], rhs=u[32 * q:32 * q + 32, :], start=True, stop=True, tile_position=(32 * q, 0)), [btb, u], [pim])
                bre = t32(); bim = t32()
                k.op(AC, lambda e, bre=bre, pre=pre: e.activation(out=bre[:], in_=pre[:], func=AF.Copy), [pre], [bre])
                k.op(AC, lambda e, bim=bim, pim=pim: e.activation(out=bim[:], in_=pim[:], func=AF.Copy), [pim], [bim])
                gre = GRE[:, st, :].unsqueeze(1).to_broadcast([128, 4, 128])
                gim = GIM[:, st, :].unsqueeze(1).to_broadcast([128, 4, 128])
                a1 = t32(); a2 = t32(); mre = t32(); mim = t32()
                k.op(DV, lambda e, a1=a1, bre=bre, gre=gre: e.tensor_tensor(out=v4(a1[:]), in0=v4(bre[:]), in1=gre, op=ALU.mult), [bre, GRE], [a1])
                k.op(DV, lambda e, a2=a2, bim=bim, gim=gim: e.tensor_tensor(out=v4(a2[:]), in0=v4(bim[:]), in1=gim, op=ALU.mult), [bim, GIM], [a2])
                k.op(DV, lambda e, a1=a1, a2=a2, mre=mre: e.tensor_tensor(out=mre[:], in0=a1[:], in1=a2[:], op=ALU.subtract), [a1, a2], [mre])
                a3 = t32(); a4 = t32()
                k.op(DV, lambda e, a3=a3, bim=bim, gre=gre: e.tensor_tensor(out=v4(a3[:]), in0=v4(bim[:]), in1=gre, op=ALU.mult), [bim, GRE], [a3])
                k.op(DV, lambda e, a4=a4, bre=bre, gim=gim: e.tensor_tensor(out=v4(a4[:]), in0=v4(bre[:]), in1=gim, op=ALU.mult), [bre, GIM], [a4])
                k.op(DV, lambda e, a3=a3, a4=a4, mim=mim: e.tensor_tensor(out=mim[:], in0=a3[:], in1=a4[:], op=ALU.add), [a3, a4], [mim])
                wr = wre(); wi = wim()
                for c in range(4):
                    cs = slice(c * 128, (c + 1) * 128)
                    k.op(DV, lambda e, wr=wr, mre=mre, cs=cs, st=st: e.tensor_tensor_scan(out=wr[:, cs], data0=RPAT[:, st, :], data1=mre[:, cs], initial=s5i[:, 0, st:st + 1],
                                                                                      op0=ALU.mult, op1=ALU.add), [RPAT, mre, s5i], [wr])
                    k.op(DV, lambda e, wi=wi, mim=mim, cs=cs, st=st: e.tensor_tensor_scan(out=wi[:, cs], data0=RPAT[:, st, :], data1=mim[:, cs], initial=s5i[:, 1, st:st + 1],
                                                                                      op0=ALU.mult, op1=ALU.add), [RPAT, mim, s5i], [wi])
                    e0 = c * 128 + 127
                    tA = s5tmp(); tB = s5tmp()
                    c128 = COS[:, st, 128:129]; s128 = SIN[:, st, 128:129]
                    k.op(DV, lambda e, tA=tA, wi=wi, e0=e0, s128=s128: e.tensor_scalar(out=tA[:], in0=wi[:, e0:e0 + 1], scalar1=s128, scalar2=None, op0=ALU.mult), [wi, SIN], [tA])
                    k.op(DV, lambda e, tB=tB, wr=wr, e0=e0, s128=s128: e.tensor_scalar(out=tB[:], in0=wr[:, e0:e0 + 1], scalar1=s128, scalar2=None, op0=ALU.mult), [wr, SIN], [tB])
                    k.op(DV, lambda e, tA=tA, wr=wr, e0=e0, c128=c128, st=st: e.scalar_tensor_tensor(out=s5i[:, 0, st:st + 1], in0=wr[:, e0:e0 + 1], scalar=c128, in1=tA[:],
                                                                                                op0=ALU.mult, op1=ALU.subtract), [wr, COS, tA], [s5i])
                    k.op(DV, lambda e, tB=tB, wi=wi, e0=e0, c128=c128, st=st: e.scalar_tensor_tensor(out=s5i[:, 1, st:st + 1], in0=wi[:, e0:e0 + 1], scalar=c128, in1=tB[:],
                                                                                                op0=ALU.mult, op1=ALU.add), [wi, COS, tB], [s5i])
                cosb = COS[:, st, 0:128].unsqueeze(1).to_broadcast([128, 4, 128])
                sinb = SIN[:, st, 0:128].unsqueeze(1).to_broadcast([128, 4, 128])
                d1 = t32(); d2 = t32(); xre = xr16(); nxi = xr16()
                DM = PL
                k.op(DM, lambda e, d1=d1, wr=wr, cosb=cosb: e.tensor_tensor(out=v4(d1[:]), in0=v4(wr[:]), in1=cosb, op=ALU.mult), [wr, COS], [d1])
                k.op(DM, lambda e, d2=d2, wi=wi, sinb=sinb: e.tensor_tensor(out=v4(d2[:]), in0=v4(wi[:]), in1=sinb, op=ALU.mult), [wi, SIN], [d2])
                k.op(DM, lambda e, d1=d1, d2=d2, xre=xre: e.tensor_tensor(out=xre[:], in0=d1[:], in1=d2[:], op=ALU.subtract), [d1, d2], [xre])
                d3 = t32(); d4 = t32()
                k.op(DM, lambda e, d3=d3, wi=wi, cosb=cosb: e.tensor_tensor(out=v4(d3[:]), in0=v4(wi[:]), in1=cosb, op=ALU.mult), [wi, COS], [d3])
                k.op(DM, lambda e, d4=d4, wr=wr, sinb=sinb: e.tensor_tensor(out=v4(d4[:]), in0=v4(wr[:]), in1=sinb, op=ALU.mult), [wr, SIN], [d4])
                k.op(DM, lambda e, d3=d3, d4=d4, nxi=nxi: e.tensor_tensor(out=nxi[:], in0=d3[:], in1=d4[:], op=ALU.add), [d3, d4], [nxi])
                if pend is not None:
                    pend()
                def mk(yps=yps, xre=xre, nxi=nxi, q=q, st=st):
                    def f():
                        k.op("pe", lambda e: e.matmul(yps[32 * q:32 * q + 32, :], lhsT=ctv[:, 0, st, :], rhs=xre[:], start=True, stop=False, tile_position=(0, 32 * q)), [ctb, xre], [yps], inc=False)
                        k.op("pe", lambda e: e.matmul(yps[32 * q:32 * q + 32, :], lhsT=ctv[:, 1, st, :], rhs=nxi[:], start=False, stop=True, tile_position=(0, 32 * q)), [ctb, nxi], [yps])
                    return f
                pend = mk()
                yield
            pend()
            yy = t32(); y2 = t32(); sg = t32()
            k.op(DV, lambda e, yy=yy, u=u, yps=yps, ct=ct: e.scalar_tensor_tensor(out=yy[:], in0=u[:], scalar=S[:, 64 + ct:65 + ct], in1=yps[:], op0=ALU.mult, op1=ALU.add), [u, S, yps], [yy])
            k.op(DV, lambda e, yy=yy, y2=y2: e.tensor_tensor(out=y2[:], in0=yy[:], in1=yy[:], op=ALU.mult), [yy], [y2])
            k.op(DV, lambda e, y2=y2: e.tensor_scalar(out=y2[:], in0=y2[:], scalar1=0.044715, scalar2=1.0, op0=ALU.mult, op1=ALU.add), [y2], [y2])
            k.op(DV, lambda e, yy=yy, y2=y2: e.tensor_tensor(out=y2[:], in0=y2[:], in1=yy[:], op=ALU.mult), [yy, y2], [y2])
            k.op(AC, lambda e, sg=sg, y2=y2: e.activation(out=sg[:], in_=y2[:], func=AF.Sigmoid, scale=float(2.0 * math.sqrt(2.0 / math.pi))), [y2], [sg])
            gq = gy[2 * (tt % 2) + ct]
            k.op(DV, lambda e, yy=yy, sg=sg, gq=gq: e.tensor_tensor(out=gq[:], in0=yy[:], in1=sg[:], op=ALU.mult), [yy, sg], [gq])
            k.dma(GYLc[cchunk][ct * 128:(ct + 1) * 128, csl_], gq[:], reads=[gq], writes=[GYLb[ct][tt]], q="pool")
            yield
        for ct in range(NH):
            cb = cxb[ct]
            k.dma(cb[:, 3:3 + TT], PJ[20 + ct][:, sl], reads=[PJb[20 + ct][tt]], writes=[cb])
            xc = t32(); xcb = t16()
            k.op(DV, lambda e, xc=xc, cb=cb, ct=ct: e.tensor_scalar(out=xc[:], in0=cb[:, 0:TT], scalar1=S[:, 69 + ct * 4:70 + ct * 4], scalar2=S[:, 85 + ct:86 + ct], op0=ALU.mult, op1=ALU.add), [cb, S], [xc])
            for kk in range(1, 4):
                k.op(DV, lambda e, xc=xc, cb=cb, ct=ct, kk=kk: e.scalar_tensor_tensor(out=xc[:], in0=cb[:, kk:kk + TT], scalar=S[:, 69 + ct * 4 + kk:70 + ct * 4 + kk], in1=xc[:], op0=ALU.mult, op1=ALU.add), [cb, S, xc], [xc])
            k.op(DV, lambda e, cb=cb: e.tensor_copy(out=cb[:, 0:3], in_=cb[:, TT:TT + 3]), [cb], [cb])
            k.op(AC, lambda e, xc=xc, xcb=xcb: e.activation(out=xcb[:], in_=xc[:], func=AF.Copy), [xc], [xcb])
            pr = psn(); pi_ = psn()
            k.op("pe", lambda e, pr=pr, xcb=xcb, ct=ct: e.matmul(pr[:], lhsT=lrub[:, ct * 128:(ct + 1) * 128], rhs=xcb[:], start=True, stop=True), [lrub, xcb], [pr])
            k.op("pe", lambda e, pi_=pi_, xcb=xcb, ct=ct: e.matmul(pi_[:], lhsT=lrub[:, 512 + ct * 128:512 + (ct + 1) * 128], rhs=xcb[:], start=True, stop=True), [lrub, xcb], [pi_])
            rg = t32(); ig = t32(); aa = t32(); om = t32()
            k.op(AC, lambda e, rg=rg, pr=pr, ct=ct: e.activation(out=rg[:], in_=pr[:], func=AF.Sigmoid, bias=S[:, 89 + ct:90 + ct], scale=1.0), [pr, S], [rg])
            k.op(AC, lambda e, ig=ig, pi_=pi_, ct=ct: e.activation(out=ig[:], in_=pi_[:], func=AF.Sigmoid, bias=S[:, 93 + ct:94 + ct], scale=1.0), [pi_, S], [ig])
            k.op(AC, lambda e, aa=aa, rg=rg, ct=ct: e.activation(out=aa[:], in_=rg[:], func=AF.Exp, scale=lsc[:, ct:ct + 1]), [rg, lsc], [aa])
            k.op(DV, lambda e, aa=aa, om=om: e.tensor_tensor(out=om[:], in0=aa[:], in1=aa[:], op=ALU.mult), [aa], [om])
            k.op(DV, lambda e, om=om: e.tensor_scalar(out=om[:], in0=om[:], scalar1=-1.0, scalar2=1.0, op0=ALU.mult, op1=ALU.add), [om], [om])
            k.op(AC, lambda e, om=om: e.activation(out=om[:], in_=om[:], func=AF.Sqrt), [om], [om])
            k.op(DV, lambda e, ig=ig, xc=xc: e.tensor_tensor(out=ig[:], in0=ig[:], in1=xc[:], op=ALU.mult), [ig, xc], [ig])
            k.op(DV, lambda e, ig=ig, om=om: e.tensor_tensor(out=ig[:], in0=ig[:], in1=om[:], op=ALU.mult), [ig, om], [ig])
            hh = t32()
            k.op(DV, lambda e, hh=hh, aa=aa, ig=ig, ct=ct: e.tensor_tensor_scan(out=hh[:], data0=aa[:], data1=ig[:], initial=lcar[:, ct:ct + 1], op0=ALU.mult, op1=ALU.add), [aa, ig, lcar], [hh])
            k.op(DV, lambda e, hh=hh, ct=ct: e.tensor_copy(out=lcar[:, ct:ct + 1], in_=hh[:, TT - 1:TT]), [hh], [lcar])
            gate_store(l, hh, hh[:], 24 + ct, 8 + ct, tt, sl)
            yield
        for hd in range(NH):
            mq = ld16()
            k.dma(mq[:], PJ[28 + hd][:, sl], reads=[PJb[28 + hd][tt]], writes=[mq])
            pes = []
            for mt in range(2):
                pss = psn()
                k.op("pe", lambda e, pss=pss, mq=mq, hd=hd, mt=mt: e.matmul(pss[:], lhsT=kmem[:, hd, mt * 128:(mt + 1) * 128], rhs=mq[:], start=True, stop=True), [kmem, mq], [pss])
                pe_ = t16()
                k.op(AC, lambda e, pe_=pe_, pss=pss: e.activation(out=pe_[:], in_=pss[:], func=AF.Exp, scale=float(128 ** -0.5)), [pss], [pe_])
                pes.append(pe_)
            po = psn(); pl = psn()
            for mt in range(2):
                k.op("pe", lambda e, po=po, pe_=pes[mt], hd=hd, mt=mt: e.matmul(po[:], lhsT=vmem[:, mt, hd * 128:(hd + 1) * 128], rhs=pe_[:], start=(mt == 0), stop=(mt == 1)), [vmem, pes[mt]], [po], inc=(mt == 1))
            for mt in range(2):
                k.op("pe", lambda e, pl=pl, pe_=pes[mt], mt=mt: e.matmul(pl[:], lhsT=ones[:], rhs=pe_[:], start=(mt == 0), stop=(mt == 1)), [ones, pes[mt]], [pl], inc=(mt == 1))
            rl = t32(); ym = t32()
            k.op(DV, lambda e, rl=rl, pl=pl: e.reciprocal(out=rl[:], in_=pl[:]), [pl], [rl])
            k.op(DV, lambda e, ym=ym, po=po, rl=rl: e.tensor_tensor(out=ym[:], in0=po[:], in1=rl[:], op=ALU.mult), [po, rl], [ym])
            gate_store(l, ym, ym[:], 32 + hd, 12 + hd, tt, sl)
            yield

    def u_ag(l, c):
        tts = range(c * CH, (c + 1) * CH)
        k.coll("AllGather", ALU.bypass, GYLc[c], GYFc[c], [GYLb[ct][t] for ct in range(NH) for t in tts], [GYFb[c]])
        yield

    def u_glu(l, tt):
        sl = slice(tt * TT, (tt + 1) * TT)
        c, cs = csl(tt)
        gk = [gyf() for _ in range(4)]
        k.dma_group([(gk[kt][:], GYFc[c][kt * 128:(kt + 1) * 128, cs], [GYFb[c]], [gk[kt]]) for kt in range(4)], key="gyfg")
        yield
        for o_ in range(NH):
            pa = psn(); pb = psn()
            for kt in range(4):
                k.op("pe", lambda e, pa=pa, kt=kt, o_=o_: e.matmul(pa[:], lhsT=wglu[:, kt, o_ * 128:(o_ + 1) * 128], rhs=gk[kt][:], start=(kt == 0), stop=(kt == 3)), [wglu, gk[kt]], [pa], inc=(kt == 3))
            for kt in range(4):
                k.op("pe", lambda e, pb=pb, kt=kt, o_=o_: e.matmul(pb[:], lhsT=wglu[:, kt, 256 + o_ * 128:256 + (o_ + 1) * 128], rhs=gk[kt][:], start=(kt == 0), stop=(kt == 3)), [wglu, gk[kt]], [pb], inc=(kt == 3))
            sgb = t32(); ya = t32()
            k.op(AC, lambda e, sgb=sgb, pb=pb: e.activation(out=sgb[:], in_=pb[:], func=AF.Sigmoid), [pb], [sgb])
            k.op(DV, lambda e, ya=ya, pa=pa, sgb=sgb: e.tensor_tensor(out=ya[:], in0=pa[:], in1=sgb[:], op=ALU.mult), [pa, sgb], [ya])
            gate_store(l, ya, ya[:], 4 + o_, o_, tt, sl)
            yield

    def u_s2a(l, tt):
        dap = dap2[l % 2]
        PJ = PJ2[l % 2]; PJb = PJb2[l % 2]; VS = VS2[l % 2]; VSb = VSb2[l % 2]
        sl = slice(tt * TT, (tt + 1) * TT)
        acc_o = [PS[4], PS[5]]
        units = []
        for kb in range(tt + 1):
            for sub in range(4):
                for c in range(2):
                    units.append((kb, sub, c, (sub * 128 if kb == tt else 0), (kb == 0 and sub == 0), (kb == tt and sub == 3)))
        qts = {}
        tiles = {}
        def load(hd, kb):
            ksl = slice(kb * TT, (kb + 1) * TT)
            kt_ = ld16A()
            k.dma(kt_[:], PJ[12 + hd][:, ksl], reads=[PJb[12 + hd][kb]], writes=[kt_])
            vb = vblk()
            k.dma(vb[:], VS[kb * TT:(kb + 1) * TT, hd * 128:(hd + 1) * 128].rearrange("(s p) d -> p s d", p=128), reads=[VSb[kb]], writes=[vb])
            tiles[(hd, kb)] = (kt_, vb)
        def load_q(hd):
            qt = qtr()
            k.dma(qt[:], PJ[8 + hd][:, sl], reads=[PJb[8 + hd][tt]], writes=[qt])
            qts[hd] = qt
        load_q(0)
        load(0, 0)
        for hd in range(NH):
            qt = qts[hd]
            def emitS(j, qt=qt, hd=hd):
                kb, sub, c, q0, first, last = units[j]
                kt_ = tiles[(hd, kb)][0]
                pss = PS[j % 2]
                k.op("pe", lambda e: e.matmul(pss[:, q0:TT], lhsT=kt_[64 * c:64 * c + 64, sub * 128:(sub + 1) * 128],
                                              rhs=qt[64 * c:64 * c + 64, q0:TT], start=True, stop=True), [kt_, qt], [pss])
            emitS(0)
            emitS(1)
            for j, (kb, sub, c, q0, first, last) in enumerate(units):
                if sub == 0 and c == 0:
                    if kb + 1 <= tt:
                        load(hd, kb + 1)
                    elif hd + 1 < NH:
                        load_q(hd + 1)
                        load(hd + 1, 0)
                vb = tiles[(hd, kb)][1]
                pss = PS[j % 2]
                pe_ = t16()
                k.op(AC, lambda e, pe_=pe_, pss=pss, q0=q0: e.activation(out=pe_[:, q0:TT], in_=pss[:, q0:TT], func=AF.Exp, scale=0.125), [pss], [pe_])
                if kb == tt:
                    k.op(DV, lambda e, pe_=pe_, q0=q0: e.tensor_tensor(out=pe_[:, q0:q0 + 128], in0=pe_[:, q0:q0 + 128], in1=tri[:], op=ALU.mult), [pe_, tri], [pe_])
                if j + 2 < len(units):
                    emitS(j + 2)
                k.op("pe", lambda e, ao=acc_o[c], pe_=pe_, vb=vb, sub=sub, q0=q0, first=first, last=last: e.matmul(ao[:, q0:TT], lhsT=vb[:, sub, :], rhs=pe_[:, q0:TT], start=first, stop=last),
                     [vb, pe_], [acc_o[c]])
                k.op("pe", lambda e, c=c, pe_=pe_, q0=q0, first=first, last=last: e.matmul(LS[32 * c:32 * c + 32, q0:TT], lhsT=ones[:, 0:32], rhs=pe_[:, q0:TT], start=first, stop=last, tile_position=(0, 32 * c)),
                     [ones, pe_], [LSb[c]])
                if c == 1:
                    yield
            l32 = fin_l
            k.op(AC, lambda e, l32=l32: e.activation(out=l32[0:64, :], in_=LS[0:64, :], func=AF.Copy), [LSb[0], LSb[1]], [l32])
            k.op(DV, lambda e, l32=l32: e.reciprocal(out=l32[0:64, :], in_=l32[0:64, :]), [l32], [l32])
            yield
            pl0 = psa(); pl1 = psa()
            k.op("pe", lambda e, pl0=pl0, l32=l32: e.matmul(pl0[:], lhsT=cn[:, 262:390], rhs=l32[:], start=True, stop=True), [cn, l32], [pl0])
            k.op("pe", lambda e, pl1=pl1, l32=l32: e.matmul(pl1[:], lhsT=cn[:, 390:518], rhs=l32[:], start=True, stop=True), [cn, l32], [pl1])
            r0 = t32(); r1 = t32(); o0 = fin_o; o1 = t32()
            k.op(AC, lambda e, r0=r0, pl0=pl0: e.activation(out=r0[:], in_=pl0[:], func=AF.Copy), [pl0], [r0])
            k.op(AC, lambda e, r1=r1, pl1=pl1: e.activation(out=r1[:], in_=pl1[:], func=AF.Copy), [pl1], [r1])
            k.op(DV, lambda e, o0=o0, r0=r0: e.tensor_tensor(out=o0[:], in0=acc_o[0][:], in1=r0[:], op=ALU.mult), [acc_o[0], r0], [o0])
            k.op(DV, lambda e, o1=o1, r1=r1: e.tensor_tensor(out=o1[:], in0=acc_o[1][:], in1=r1[:], op=ALU.mult), [acc_o[1], r1], [o1])
            k.op(DV, lambda e, o0=o0, o1=o1: e.scalar_tensor_tensor(out=o0[:], in0=o1[:], scalar=dap[:, 4:5], in1=o0[:], op0=ALU.mult, op1=ALU.add), [o0, o1, dap], [o0])
            sq_ = fin_sq
            k.op(AC, lambda e, sq_=sq_, o0=o0: e.activation(out=sq_[:], in_=o0[:], func=AF.Square), [o0], [sq_])
            yield
            pn2 = psa()
            k.op("pe", lambda e, pn2=pn2, sq_=sq_: e.matmul(pn2[:], lhsT=ones[:], rhs=sq_[:], start=True, stop=True), [ones, sq_], [pn2])
            rr2 = rms_rstd(pn2, TT, 128)
            yb_ = t32()
            k.op(DV, lambda e, yb_=yb_, o0=o0, rr2=rr2: e.scalar_tensor_tensor(out=yb_[:], in0=o0[:], scalar=dap[:, 5:6], in1=rr2[:], op0=ALU.mult, op1=ALU.mult), [o0, dap, rr2], [yb_])
            gate_store(l, yb_, yb_[:], 16 + hd, 4 + hd, tt, sl)
            yield

    MIXK = [0, 1, 4, 5, 8, 9, 12, 13]

    def u_s3(l, tt):
        MX = MX2[l % 2]; MXb = MXb2[l % 2]
        sl = slice(tt * TT, (tt + 1) * TT)
        c, cs = csl(tt)
        mm = [mxs[8 * (tt % 2) + j] for j in range(8)]
        k.dma_group([(mm[j][:], MX[kt][:, sl], [MXb[kt][tt]], [mm[j]]) for j, kt in enumerate(MIXK)], key=f"mxg{tt % 2}")
        yield
        for ob in range(8):
            wo = wo_s()
            k.dma(wo[:], WB[l, 24 + ob], reads=[WBb[l][24 + ob]], writes=[wo])
            pp = psn()
            for j in range(8):
                k.op("pe", lambda e, pp=pp, wo=wo, j=j: e.matmul(pp[:], lhsT=wo[:, j * 128:(j + 1) * 128], rhs=mm[j][:], start=(j == 0), stop=(j == 7)), [wo, mm[j]], [pp], inc=(j == 7))
            xo = st32()
            if l == 0:
                k.dma(xo[:], xT[ob * 128:(ob + 1) * 128, sl], writes=[xo])
            else:
                k.dma(xo[:], XRc[c][ob * 128:(ob + 1) * 128, cs], reads=[XRb[ob][tt]], writes=[xo])
            k.op(DV, lambda e, xo=xo, pp=pp: e.scalar_tensor_tensor(out=xo[:], in0=xo[:], scalar=0.5, in1=pp[:], op0=ALU.mult, op1=ALU.add), [xo, pp], [xo])
            k.dma(PTc[c][ob * 128:(ob + 1) * 128, cs], xo[:], reads=[xo], writes=[PTb[ob][tt]], q="pool")
            yield

    def u_ar(l, c):
        tts = range(c * CH, (c + 1) * CH)
        for hf in range(2):
            obs = range(4 * hf, 4 * hf + 4)
            k.coll("AllReduce", ALU.add, PTc[c][512 * hf:512 * (hf + 1), :], XRc[c][512 * hf:512 * (hf + 1), :],
                   [PTb[ob][t] for ob in obs for t in tts], [XRb[ob][t] for ob in obs for t in tts])
        yield

    def u_final(tt):
        sl = slice(tt * TT, (tt + 1) * TT)
        x_ = xs()
        k.dma(x_[:], XRv(tt), reads=[XRb[kt][tt] for kt in range(8)], writes=[x_])
        for kt in range(8):
            k.op(AC, lambda e, kt=kt: e.activation(out=sqb[:, kt, :], in_=x_[:, kt, :], func=AF.Square), [x_], [sqb])
        pn = psn()
        for kt in range(8):
            k.op("pe", lambda e, kt=kt: e.matmul(pn[:], lhsT=ones[:], rhs=sqb[:, kt, :], start=(kt == 0), stop=(kt == 7)), [ones, sqb], [pn], inc=(kt == 7))
        rr = rms_rstd(pn, TT, D)
        for kt in range(8):
            k.op(DV, lambda e, kt=kt: e.scalar_tensor_tensor(out=x_[:, kt, :], in0=x_[:, kt, :], scalar=fnt[:, kt:kt + 1], in1=rr[:],
                                                            op0=ALU.mult, op1=ALU.mult), [x_, fnt, rr], [x_])
        k.dma(yTv[:, :, sl], x_[:], reads=[x_], writes=[YB], q="pool")
        yield

    LT = NT - 1
    sS1 = [("init",)]
    for l in range(depth):
        sS1.append(("conv", l))
    for l in range(depth):
        sS1.append(("setupA", l))
        if l == 0:
            sS1.append(("s1m", 0))
        sS1 += [("s1", l, tt) for tt in range(NT)]
        if l > 0:
            sS1.append(("s1m", l))
    sS1 += [("final", tt) for tt in range(NT)]
    sA = [("s2a", l, tt) for l in range(depth) for tt in range(NT)]
    sB = []
    sG = []
    sC = []
    for l in range(depth):
        sB.append(("setupB", l))
        for tt in range(NT):
            sB.append(("s2b", l, tt))
            sG.append(("glu", l, tt))
            sC.append(("s3", l, tt))
            if tt % CH == CH - 1:
                sB.append(("ag", l, tt // CH))
                sC.append(("ar", l, tt // CH))

    def deps(u):
        t = u[0]
        if t == "s1":
            return [("ar", u[1] - 1, u[2] // CH)] if u[1] > 0 else []
        if t == "s1m":
            return [("s2b", u[1] - 1, LT)] if u[1] > 0 else []
        if t == "setupA":
            return [("s2a", u[1] - 2, LT), ("s2b", u[1] - 2, LT)] if u[1] >= 2 else []
        if t == "final":
            return [("ar", depth - 1, u[1] // CH)]
        if t == "setupB":
            d = [("setupA", u[1])]
            if u[1] > 0:
                d += [("s2b", u[1] - 1, LT), ("glu", u[1] - 1, LT)]
            if u[1] > 1:
                d += [("s2a", u[1] - 2, LT)]
            return d
        if t == "s2b":
            return [("s1", u[1], u[2]), ("s1m", u[1]), ("setupB", u[1])]
        if t == "s2a":
            return [("s1", u[1], u[2]), ("setupB", u[1])]
        if t == "glu":
            return [("ag", u[1], u[2] // CH), ("setupB", u[1])]
        if t == "s3":
            return [("s2a", u[1], u[2]), ("s2b", u[1], u[2]), ("glu", u[1], u[2])]
        return []

    def make(u):
        t = u[0]
        return {"init": u_init, "conv": u_conv, "setupA": u_setupA, "s1m": u_s1m, "s1": u_s1, "final": u_final,
                "setupB": u_setupB, "s2b": u_s2b, "s2a": u_s2a, "s3": u_s3, "glu": u_glu, "ag": u_ag, "ar": u_ar}[t](*u[1:])

    done = set()
    streams = [[sS1, 0, None], [sA, 0, None], [sB, 0, None], [sG, 0, None], [sC, 0, None]]
    weights = [1, 4, 1, 1, 1]
    while True:
        progressed = False
        alldone = True
        for si, stt in enumerate(streams):
            lst, idx, gen = stt
            for _ in range(weights[si]):
                lst, idx, gen = stt
                if gen is None:
                    if idx >= len(lst):
                        break
                    u = lst[idx]
                    if all(d in done for d in deps(u)):
                        stt[2] = make(u)
                        gen = stt[2]
                    else:
                        break
                try:
                    next(gen)
                    progressed = True
                except StopIteration:
                    done.add(lst[idx])
                    stt[1] = idx + 1
                    stt[2] = None
                    progressed = True
            if stt[1] < len(stt[0]):
                alldone = False
        if alldone:
            break
        assert progressed, "scheduler stuck"
    k.finish([YB])
    return nc, k


_CACHE = {}


def run_model(inp, L, depth, ncores=8):
    packs = [pack_weights(inp, depth, h) for h in range(2)]
    if (L, depth) not in _CACHE:
        _CACHE[(L, depth)] = build(L, depth)[0]
    nc = _CACHE[(L, depth)]
    B = inp["x"].shape[0]
    in_maps = []
    for c in range(ncores):
        b = (c // 2) % B
        WALL, SP, CN, FN = packs[c % 2]
        in_maps.append({
            "xT": np.ascontiguousarray(np.asarray(inp["x"][b], np.float32).T),
            "memT": np.ascontiguousarray(np.asarray(inp["mem"][b], np.float32).T),
            "pos": np.ascontiguousarray(np.asarray(inp["positions"][b], np.int32).reshape(1, L)),
            "wall": WALL, "sp": SP, "cn": CN, "fn": FN,
        })
    res = run_bass_kernel_spmd(nc, in_maps, core_ids=list(range(ncores)))
    out = np.stack([np.ascontiguousarray(res.results[2 * b]["yT"].T) for b in range(B)], axis=0)
    return out.astype(np.float32)


def kernel(**inputs):
    return run_model(inputs, 8192, 4, 8)
```
